# Optimizing a Trainium2 kernel written in Bass

```python
import jax, jax.numpy as jnp
from jax import lax
import numpy as np

D_MODEL = 1024
BATCH = 2
SEQ = 8192
DEPTH = 1

GRID_W = 64
CTX_LEN = 256
D_MIX = D_MODEL
D_FOURIER = D_MIX // 2
D_LRU = D_MIX - D_FOURIER
N_FOURIER_GROUPS = 4
FOURIER_GROUP_DIM = D_FOURIER // N_FOURIER_GROUPS
N_LRU_HEADS = 4
LRU_HEAD_DIM = D_LRU // N_LRU_HEADS
CONV_WIDTH = 4
CONV_PAD = (2, 1)
LRU_C = 8.0
EPS = 1e-6

kernel_name = "hybrid_fourier_rglru_prefix_dit_layer"


def rmsnorm(x, g):
    xf = x.astype(jnp.float32)
    y = xf * lax.rsqrt(jnp.mean(xf * xf, axis=-1, keepdims=True) + EPS)
    return (y * g.astype(jnp.float32)).astype(x.dtype)


def modulate(x, g, shift, scale):
    return rmsnorm(x, g) * (1.0 + scale) + shift


def in_proj(h, w_in):
    z = h @ w_in
    return jnp.split(z, [D_FOURIER, 2 * D_FOURIER, 2 * D_FOURIER + D_LRU], axis=-1)


def fourier_branch(u, w_four):
    b_, l_, _ = u.shape
    ug = u.astype(jnp.float32).reshape(b_, l_, N_FOURIER_GROUPS, FOURIER_GROUP_DIM)
    mixed = jnp.fft.fftn(ug, axes=(1, 3), norm="ortho").real
    return mixed.reshape(b_, l_, D_FOURIER).astype(u.dtype) @ w_four


def dwconv_centred(u, w, b):
    y = lax.conv_general_dilated(u, w[:, None, :].astype(u.dtype), window_strides=(1,),
                                 padding=[CONV_PAD],
                                 dimension_numbers=("NWC", "WIO", "NWC"),
                                 feature_group_count=D_LRU)
    return y + b


def block_diag(u, w, b):
    b_, l_, _ = u.shape
    uh = u.reshape(b_, l_, N_LRU_HEADS, LRU_HEAD_DIM)
    return jnp.einsum("blhd,hde->blhe", uh, w).reshape(b_, l_, D_LRU) + b


def _combine(left, right):
    a1, b1 = left
    a2, b2 = right
    return a1 * a2, a2 * b1 + b2


def rglru_direction(u, w_rg, b_rg, w_ig, b_ig, lam, h0, reverse):
    r = jax.nn.sigmoid(block_diag(u, w_rg, b_rg).astype(jnp.float32))
    i = jax.nn.sigmoid(block_diag(u, w_ig, b_ig).astype(jnp.float32))
    log_a = -LRU_C * r * jax.nn.softplus(-lam.astype(jnp.float32))
    a = jnp.exp(log_a)
    bx = jnp.sqrt(-jnp.expm1(2.0 * log_a)) * (i * u.astype(jnp.float32))
    a_cum, b_cum = lax.associative_scan(_combine, (a, bx), axis=1, reverse=reverse)
    h = a_cum * h0[:, None, :] + b_cum
    final = h[:, 0] if reverse else h[:, -1]
    return h, final


def lru_branch(u, conv_w, conv_b, w_rg, b_rg, w_ig, b_ig, lam, h0_fwd, h0_bwd):
    v = dwconv_centred(u, conv_w, conv_b)
    h_f, fin_f = rglru_direction(v, w_rg[0], b_rg[0], w_ig[0], b_ig[0], lam[0], h0_fwd, False)
    h_b, fin_b = rglru_direction(v, w_rg[1], b_rg[1], w_ig[1], b_ig[1], lam[1], h0_bwd, True)
    return (h_f + h_b).astype(u.dtype), fin_f, fin_b


def out_proj(y_f, g_f, y_l, g_l, w_out):
    y = jnp.concatenate([y_f * jax.nn.silu(g_f), y_l * jax.nn.silu(g_l)], axis=-1)
    return y @ w_out


def setup_inputs(seed: int = 0) -> dict:
    key = jax.random.key(seed)
    ks = jax.random.split(key, 20)
    f32 = jnp.float32
    x = jax.random.normal(ks[0], (BATCH, SEQ, D_MODEL), f32)
    c = jax.random.normal(ks[1], (BATCH, D_MODEL), f32)
    ctx = jax.random.normal(ks[2], (BATCH, CTX_LEN, D_MODEL), f32)
    c_ctx = jax.random.normal(ks[3], (D_MODEL,), f32)
    w_ada = jax.random.normal(ks[4], (DEPTH, D_MODEL, 3 * D_MODEL), f32) * (0.5 * D_MODEL ** -0.5)
    b_ada = jax.random.normal(ks[5], (DEPTH, 3 * D_MODEL), f32) * 0.02
    norm_gain = 1.0 + 0.05 * jax.random.normal(ks[6], (DEPTH, D_MODEL), f32)
    w_in = jax.random.normal(ks[7], (DEPTH, D_MODEL, 2 * D_MIX), f32) * D_MODEL ** -0.5
    w_four = jax.random.normal(ks[8], (DEPTH, D_FOURIER, D_FOURIER), f32) * D_FOURIER ** -0.5
    conv_w = jax.random.normal(ks[9], (DEPTH, CONV_WIDTH, D_LRU), f32) * CONV_WIDTH ** -0.5
    conv_b = jax.random.normal(ks[10], (DEPTH, D_LRU), f32) * 0.02
    gshape = (DEPTH, 2, N_LRU_HEADS, LRU_HEAD_DIM, LRU_HEAD_DIM)
    w_rg = jax.random.normal(ks[11], gshape, f32) * LRU_HEAD_DIM ** -0.5
    b_rg = jax.random.normal(ks[12], (DEPTH, 2, D_LRU), f32) * 0.02
    w_ig = jax.random.normal(ks[13], gshape, f32) * LRU_HEAD_DIM ** -0.5
    b_ig = jax.random.normal(ks[14], (DEPTH, 2, D_LRU), f32) * 0.02
    u = jax.random.uniform(ks[15], (DEPTH, 2, D_LRU), f32, minval=0.9, maxval=0.999)
    p = u ** (1.0 / LRU_C)
    lam = jnp.log(p) - jnp.log1p(-p)
    w_out = jax.random.normal(ks[16], (DEPTH, D_MIX, D_MODEL), f32) * D_MIX ** -0.5
    final_gain = 1.0 + 0.05 * jax.random.normal(ks[17], (D_MODEL,), f32)
    return {"x": x, "c": c, "ctx": ctx, "c_ctx": c_ctx, "w_ada": w_ada, "b_ada": b_ada,
            "norm_gain": norm_gain, "w_in": w_in, "w_four": w_four, "conv_w": conv_w,
            "conv_b": conv_b, "w_rg": w_rg, "b_rg": b_rg, "w_ig": w_ig, "b_ig": b_ig,
            "lam": lam, "w_out": w_out, "final_gain": final_gain}


def reference(x, c, ctx, c_ctx, w_ada, b_ada, norm_gain, w_in, w_four, conv_w, conv_b,
              w_rg, b_rg, w_ig, b_ig, lam, w_out, final_gain):
    bsz = x.shape[0]
    for l in range(DEPTH):
        mod = jax.nn.silu(c) @ w_ada[l] + b_ada[l]
        shift, scale, gate = jnp.split(mod[:, None, :], 3, axis=-1)
        mod_c = jax.nn.silu(c_ctx) @ w_ada[l] + b_ada[l]
        shift_c, scale_c, gate_c = jnp.split(mod_c, 3)

        hx = modulate(x, norm_gain[l], shift, scale)
        hc = modulate(ctx, norm_gain[l], shift_c, scale_c)
        uf_x, gf_x, ul_x, gl_x = in_proj(hx, w_in[l])
        uf_c, gf_c, ul_c, gl_c = in_proj(hc, w_in[l])

        zeros = jnp.zeros((bsz, D_LRU), jnp.float32)
        yl_c, fin_f, fin_b = lru_branch(ul_c, conv_w[l], conv_b[l], w_rg[l], b_rg[l],
                                        w_ig[l], b_ig[l], lam[l], zeros, zeros)
        yl_x, _, _ = lru_branch(ul_x, conv_w[l], conv_b[l], w_rg[l], b_rg[l],
                                w_ig[l], b_ig[l], lam[l], fin_f, fin_b)
        yf_x = fourier_branch(uf_x, w_four[l])
        x = x + gate * out_proj(yf_x, gf_x, yl_x, gl_x, w_out[l])
        if l < DEPTH - 1:
            yf_c = fourier_branch(uf_c, w_four[l])
            ctx = ctx + gate_c * out_proj(yf_c, gf_c, yl_c, gl_c, w_out[l])
    return rmsnorm(x, final_gain)
```

```python
import numpy as np
import ml_dtypes
from contextlib import ExitStack
import concourse.bass as bass
import concourse.mybir as mybir
from concourse.bass_utils import run_bass_kernel_spmd

F32 = mybir.dt.float32
BF16 = mybir.dt.bfloat16
AF = mybir.ActivationFunctionType
ALU = mybir.AluOpType

D = 1024
SEQ = 8192
CTX = 256
OWN = 2048
EPS = 1e-6

T_ANY, T_SILU, T_EXP, T_SQRT = 0, 1, 2, 3


class Buf:
    def __init__(self, name):
        self.name = name
        self.writers = []
        self.readers = []
        self.war = []


class Node:
    __slots__ = ("id", "eng", "fn", "deps", "dur", "kind", "key", "nbytes", "tset", "seq", "semval", "start", "end")


def act_ns(n):
    return 220 + 0.78 * n


def dve_ns(n, c=1.0):
    return 150 + c * n


def pool_ns(n, c=1.8):
    return 150 + c * n


def pe_ns(n, f=1):
    return max(120.0, 60 + 0.56 * n) * f


class Prog:
    ENGS = ["sync", "scalar", "vector", "gpsimd", "tensor"]

    def __init__(self, nc, stack):
        self.nc = nc
        self.stack = stack
        self.nodes = []

    def _deps(self, reads, writes, pwrites, extra):
        deps = set(x for x in extra if x is not None)
        for b in reads:
            deps.update(b.writers)
        for b in writes:
            deps.update(b.writers)
            deps.update(b.readers)
            deps.update(b.war)
        for b in pwrites:
            deps.update(b.war)
        return deps

    def _mark(self, nid, reads, writes, pwrites):
        for b in reads:
            b.readers.append(nid)
        for b in writes:
            b.writers = [nid]
            b.readers = []
            b.war = []
        for b in pwrites:
            b.writers.append(nid)

    def begin(self, b):
        b.war = list(b.writers) + list(b.readers) + list(b.war)
        b.writers = []
        b.readers = []

    def _add(self, eng, fn, deps, dur, kind, key=None, nbytes=0, tset=T_ANY):
        n = Node()
        n.id = len(self.nodes)
        n.eng = eng
        n.fn = fn
        n.deps = deps
        n.dur = dur
        n.kind = kind
        n.key = key
        n.nbytes = nbytes
        n.tset = tset
        self.nodes.append(n)
        return n.id

    def ins(self, eng, fn, reads=(), writes=(), pwrites=(), extra=(), dur=300.0, tset=T_ANY):
        nid = self._add(eng, fn, self._deps(reads, writes, pwrites, extra), dur, "ins", tset=tset)
        self._mark(nid, reads, writes, pwrites)
        return nid

    def dma(self, key, out, in_, reads=(), writes=(), pwrites=(), extra=(), nbytes=0, dyn=None, eng="sync"):
        if dyn is None:
            fn = lambda e, out=out, in_=in_: e.dma_start(out=out, in_=in_)
        else:
            def fn(e, dyn=dyn):
                o, i = dyn(e)
                return e.dma_start(out=o, in_=i)
        nid = self._add(eng, fn, self._deps(reads, writes, pwrites, extra), 60.0, "dma", key=key, nbytes=nbytes)
        self._mark(nid, reads, writes, pwrites)
        return nid

    def cc(self, key, fn, reads=(), writes=(), extra=(), dur=50000.0):
        nid = self._add("gpsimd", fn, self._deps(reads, writes, (), extra), dur, "cc", key=key)
        self._mark(nid, reads, writes, ())
        return nid

    def schedule(self):
        nodes = self.nodes
        N = len(nodes)
        succ = [[] for _ in range(N)]
        npred = [0] * N
        for n in nodes:
            npred[n.id] = len(n.deps)
            for d in n.deps:
                succ[d].append(n.id)
        cand = {e: [] for e in self.ENGS}
        ready_t = [0.0] * N
        for n in nodes:
            if npred[n.id] == 0:
                cand[n.eng].append(n.id)
        free_at = {e: 0.0 for e in self.ENGS}
        cur_tset = T_ANY
        dma_pipe = 0.0
        cc_pipe = 0.0
        order = {e: [] for e in self.ENGS}
        done = 0
        BW = 190.0
        while done < N:
            best = None
            for e in self.ENGS:
                cl = cand[e]
                if not cl:
                    continue
                now = free_at[e]
                rdy = [i for i in cl if ready_t[i] <= now]
                if rdy:
                    if e == "scalar":
                        comp = [i for i in rdy if nodes[i].tset == T_ANY or nodes[i].tset == cur_tset]
                        pick = min(comp) if comp else min(rdy)
                    else:
                        pick = min(rdy)
                    st = now
                else:
                    pick = min(cl, key=lambda i: (ready_t[i], i))
                    st = ready_t[pick]
                if best is None or st < best[0] or (st == best[0] and pick < best[2]):
                    best = (st, e, pick)
            st, e, pick = best
            n = nodes[pick]
            cand[e].remove(pick)
            dur = n.dur
            if e == "scalar" and n.tset != T_ANY and n.tset != cur_tset:
                dur += 1300.0
                cur_tset = n.tset
            n.start = st
            if n.kind == "dma":
                free_at[e] = st + dur
                dma_pipe = max(dma_pipe, st) + n.nbytes / BW
                n.end = dma_pipe + 2000.0
            elif n.kind == "cc":
                free_at[e] = st + 100.0
                cc_pipe = max(cc_pipe, st) + dur
                n.end = cc_pipe
            else:
                free_at[e] = st + dur
                n.end = st + dur + 60.0
            order[e].append(pick)
            done += 1
            for s in succ[pick]:
                npred[s] -= 1
                if n.end > ready_t[s]:
                    ready_t[s] = n.end
                if npred[s] == 0:
                    cand[nodes[s].eng].append(s)
        self.sim_end = max(n.end for n in nodes)
        return order

    def finish(self, final_ids):
        nc = self.nc
        order = self.schedule()
        nodes = self.nodes
        sem = {e: self.stack.enter_context(nc.semaphore("done_" + e)) for e in self.ENGS}
        dsem = {}
        dcnt = {}
        for e in self.ENGS:
            k = 0
            for i in order[e]:
                n = nodes[i]
                if n.kind == "ins":
                    k += 1
                    n.seq = k
        for e in self.ENGS:
            for i in order[e]:
                n = nodes[i]
                if n.kind in ("dma", "cc"):
                    if n.key not in dsem:
                        dsem[n.key] = self.stack.enter_context(nc.semaphore("d_" + n.key))
                        dcnt[n.key] = 0
                    dcnt[n.key] += 16 if n.kind == "dma" else 1
                    n.semval = dcnt[n.key]
        waited = {}
        prog = {e: [] for e in self.ENGS}
        for e in self.ENGS:
            for i in order[e]:
                n = nodes[i]
                need = {}
                for d in n.deps:
                    dn = nodes[d]
                    if dn.kind == "ins":
                        if dn.eng == e and e == "tensor":
                            continue
                        s_, v_ = sem[dn.eng], dn.seq
                    else:
                        s_, v_ = dsem[dn.key], dn.semval
                    k_ = id(s_)
                    if k_ not in need or need[k_][1] < v_:
                        need[k_] = (s_, v_)
                waits = []
                for k_, (s_, v_) in need.items():
                    if waited.get((e, k_), 0) >= v_:
                        continue
                    waited[(e, k_)] = v_
                    waits.append((s_, v_))
                prog[e].append((n, waits))
        fin = []
        for i in final_ids:
            dn = nodes[i]
            fin.append((dsem[dn.key], dn.semval))

        def run(e, name):
            for n, waits in prog[name]:
                for s_, v_ in waits:
                    e.wait_ge(s_, v_)
                if n.kind == "ins":
                    n.fn(e).then_inc(sem[name], 1)
                elif n.kind == "dma":
                    n.fn(e).then_inc(dsem[n.key], 16)
                else:
                    n.fn(e).then_inc(dsem[n.key], 1)
            if name == "sync":
                for s_, v_ in fin:
                    e.wait_ge(s_, v_)

        with nc.Block() as block:
            @block.sync
            def _(e):
                run(e, "sync")

            @block.scalar
            def _(e):
                run(e, "scalar")

            @block.vector
            def _(e):
                run(e, "vector")

            @block.gpsimd
            def _(e):
                run(e, "gpsimd")

            @block.tensor
            def _(e):
                run(e, "tensor")


class _Stop(Exception):
    pass


def build_program(stop=99):
    nc = bass.Bass("TRN2", target_bir_lowering=False)
    dti = lambda n, s, d: nc.dram_tensor(n, s, d, kind="ExternalInput").ap()
    xb = dti("xb", [SEQ, D], F32)
    ctxb = dti("ctxb", [CTX, D], F32)
    cvec2 = dti("cvec2", [128, 16], F32)
    wada_s = dti("wada_s", [D, 768], F32)
    bada_s = dti("bada_s", [128, 12], F32)
    cm_in = nc.dram_tensor("cm_in", [128, 16], F32, kind="Internal").ap()
    cm_out = nc.dram_tensor("cm_out", [512, 16], F32, kind="Internal").ap()
    gaincol = dti("gaincol", [128, 8], F32)
    w_in_c = dti("w_in_c", [D, 896], F32)
    w_four = dti("w_four", [512, 512], F32)
    w_out = dti("w_out", [D, D], F32)
    lruvec = dti("lruvec", [128, 16], F32)
    w_gate = dti("w_gate", [128, 512], F32)
    fgrow = dti("fgrow", [1, D], F32)
    ident_d = dti("ident", [128, 128], BF16)
    f128_d = dti("f128", [128, 256], BF16)
    f64_d = dti("f64", [128, 128], BF16)
    gtab_d = dti("gtab", [128, 2 * SEQ], BF16)
    y = nc.dram_tensor("y", [OWN, D], F32, kind="ExternalOutput").ap()
    cpa_in = nc.dram_tensor("cpa_in", [128, 4096], BF16, kind="Internal").ap()
    cpa_out = nc.dram_tensor("cpa_out", [256, 4096], BF16, kind="Internal").ap()
    c4_in = [nc.dram_tensor("c4_in%d" % i, [256, 1024], BF16, kind="Internal").ap() for i in range(2)]
    c4_out = [nc.dram_tensor("c4_out%d" % i, [1024, 1024], BF16, kind="Internal").ap() for i in range(2)]
    cpm_in = nc.dram_tensor("cpm_in", [128, 4096], BF16, kind="Internal").ap()
    cpm_out = nc.dram_tensor("cpm_out", [256, 4096], BF16, kind="Internal").ap()
    c4m_in = [nc.dram_tensor("c4m_in%d" % i, [256, 1024], BF16, kind="Internal").ap() for i in range(2)]
    c4m_out = [nc.dram_tensor("c4m_out%d" % i, [1024, 1024], BF16, kind="Internal").ap() for i in range(2)]

    try:
      with ExitStack() as st:
        P = Prog(nc, st)
        state = {}

        def checkpoint(k, dumps):
            if stop != k:
                return
            bar = [n.id for n in P.nodes]
            fins = []
            for name, ap, shape, dt_ in dumps:
                d = nc.dram_tensor("dbg_" + name, shape, dt_, kind="ExternalOutput").ap()
                fins.append(P.dma("dbg_" + name, d, ap, extra=bar, nbytes=1 << 20))
            P.finish(fins)
            raise _Stop()

        sb = lambda name, shape, dt: st.enter_context(nc.sbuf_tensor(name, shape, dt))
        R1 = 32776
        R1N = 20480
        R3 = R1 + R1N
        RM = R3 + 16384
        ABF = sb("abf", [128, RM + 8192], BF16)
        AF32 = sb("af32", [128, 8192], F32)
        small = sb("small", [128, 256], F32)
        small2 = sb("small2", [128, 192], F32)
        identf = sb("identf", [128, 128], F32)
        ident = sb("ident_sb", [128, 128], BF16)
        f128 = sb("f128_sb", [128, 256], BF16)
        f64 = sb("f64_sb", [128, 128], BF16)
        dconv = sb("dconv", [128, 512], BF16)
        wgate = sb("wgate_sb", [128, 512], BF16)
        junk2 = sb("junk", [128, 2048], BF16)
        JB = [Buf("junk0"), Buf("junk1")]
        jcnt = [0]

        def next_junk():
            i = jcnt[0] % 2
            jcnt[0] += 1
            return junk2[:, i * 1024:(i + 1) * 1024], JB[i]
        upc = sb("upc", [128, 264], BF16)
        ones = sb("ones", [1, 128], F32)
        bc = sb("bc", [128, 2 * D], F32)
        vbfs = sb("vbfs", [128, 4 * 512], BF16)

        ufT = ABF[:, 0:8192]
        upad = ABF[:, 8192:16392]
        sgl = ABF[:, 16392:24584]
        sgf = ABF[:, 24584:32776]
        Wp = ABF[:, R1:R1 + 7168].rearrange("p (k n) -> p k n", k=8)
        Wpc = ABF[:, R1 + 7168:R1 + 8192].rearrange("p (k n) -> p k n", k=8)
        xn = [ABF[:, R1 + 8192 + i * 1024:R1 + 8192 + (i + 1) * 1024] for i in range(4)]
        xT = [ABF[:, R1 + 12288 + i * 4096:R1 + 12288 + (i + 1) * 4096].rearrange("p (k n) -> p k n", k=8)
              for i in range(2)]
        gtab = ABF[:, R1:R1 + 16384]
        woutp = ABF[:, R1:R1 + 8192].rearrange("p (k n) -> p k n", k=8)
        wfourp = ABF[:, R1 + 8192:R1 + 10240].rearrange("p (k n) -> p k n", k=4)
        yfg = ABF[:, R1 + 10240:R1 + 18432].rearrange("p (k n) -> p k n", k=4)
        hst = ABF[:, R3:R3 + 8192]
        Z0 = ABF[:, R3:R3 + 16384]
        Yb = ABF[:, RM:RM + 8192]
        rows = ABF[0:1, R3:R3 + 4096].bitcast(F32)
        ylgall = ABF[:, R3:R3 + 8192].rearrange("p (k n) -> p k n", k=4)
        mixall = ABF[:, R3 + 8192:R3 + 16384].rearrange("p (k n) -> p k n", k=4)
        mixT = ufT

        Q = [AF32[:, i * 2048:(i + 1) * 2048] for i in range(4)]
        QB = [Buf("q%d" % i) for i in range(4)]

        cv = small[:, 0:16]
        sc = small[:, 16:32]
        bada_sb = small[:, 32:64]
        modT = small[:, 64:96]
        gain_sb = small[:, 96:104]
        Gx = small[:, 104:112]
        Gc = small[:, 112:120]
        lv = small[:, 120:136]
        bias_sb = small[:, 136:150]
        mhalf = small[:, 154:155]
        mhalf2 = small[:, 154:156]
        sp_t = small[:, 160:176]
        negKh = small[:, 176:178]
        hbr = small[:, 178:180]
        hbi = small[:, 180:182]
        h0 = small[:, 182:184]
        zero1 = small[:, 184:185]
        negK = small[:, 186:188]
        carry = small[:, 190:198]
        ssq = [small[:, 200 + 2 * i:202 + 2 * i] for i in range(4)]
        rstd = [small[:, 208 + 2 * i:210 + 2 * i] for i in range(4)]
        ss5 = [small[:, 216 + i:217 + i] for i in range(2)]
        rs5 = [small[:, 218 + i:219 + i] for i in range(2)]

        cv4 = small2[:, 0:32]
        sc4 = small2[:, 32:64]
        badas = small2[:, 64:76]
        modS = small2[:, 76:92]
        modX = small2[:, 96:120]
        modC = small2[:, 120:144]
        sh2 = small2[:, 144:160]
        pb = [st.enter_context(nc.psum_tensor("pb%d" % i, [128, 512], F32)) for i in range(8)]
        PB = [Buf("pb%d" % i) for i in range(8)]

        names = ["ufT", "sgf", "Wp", "Wpc", "xn0", "xn1", "xn2", "xn3", "xT0", "xT1",
                 "small", "consts", "dconv", "wgate", "upc", "rows", "ones", "bc", "Z0", "Y",
                 "gtab", "woutp", "wfourp", "modT", "modS", "modX", "modC", "sh2", "s2in", "identf", "bias", "h0", "lruc", "mh",
                 "ssq0", "ssq1", "ssq2", "ssq3", "rstd0", "rstd1", "rstd2", "rstd3", "ss50", "ss51", "rs50", "rs51",
                 "vbf0", "vbf1", "vbf2", "vbf3", "xo0", "xo1", "res0", "res1"]
        B = {n: Buf(n) for n in names}
        UP = [Buf("upad%d" % i) for i in range(16)]
        SG = [Buf("sgl%d" % i) for i in range(8)]
        HS = [Buf("hst%d" % i) for i in range(16)]
        CAR = [Buf("car%d" % i) for i in range(8)]

        def mm_group(bank_buf, steps, first_extra=(), reads=(), dur=None):
            prev = None
            for si, fn in enumerate(steps):
                if si == 0:
                    prev = P.ins("tensor", fn, reads=reads, writes=[bank_buf], extra=first_extra, dur=dur)
                else:
                    prev = P.ins("tensor", fn, extra=[prev], dur=dur)
                    for b in reads:
                        b.readers.append(prev)
            bank_buf.writers = [prev]
            return prev

        def mm_chain_shared(bank_buf, steps, reads, dur):
            prev_w = list(bank_buf.writers)
            tmp = Buf("tmp")
            tmp.writers = prev_w
            tmp.readers = list(bank_buf.readers)
            last = mm_group(tmp, steps, reads=reads, dur=dur)
            bank_buf.writers = [last]
            return last

        cids = [P.dma("c0", ident[:], ident_d[:, :], nbytes=32768),
                P.dma("c1", f128[:], f128_d[:, :], nbytes=65536),
                P.dma("c2", f64[:], f64_d[:, :], nbytes=32768)]
        B["consts"].writers = list(cids)
        B["s2in"].writers = [P.dma("c3", cv4[:, 0:16], cvec2[:, :], nbytes=8192),
                             P.dma("c4", badas, bada_s[:, :], nbytes=8192)]
        sids = [P.dma("c5", gain_sb, gaincol[:, :], nbytes=4096),
                P.dma("c6", lv, lruvec[:, :], nbytes=8192)]
        B["small"].writers = list(sids)
        rids = [P.dma("c8", rows[:, D:2 * D], fgrow[:, :], nbytes=4096)]
        B["rows"].writers = list(rids)
        P.ins("gpsimd", lambda e: e.memset(mhalf2, -0.5), pwrites=[B["mh"]], dur=100)
        P.ins("gpsimd", lambda e: e.memset(zero1, 0.0), pwrites=[B["mh"]], dur=100)
        P.ins("gpsimd", lambda e: e.memset(ones[:], 1.0), writes=[B["ones"]], dur=100)
        P.ins("gpsimd", lambda e: e.memset(upad[:, 0:2], 0.0), pwrites=[UP[0]], dur=100)
        P.ins("gpsimd", lambda e: e.memset(upad[:, 8194:8200], 0.0), pwrites=[UP[15]], dur=100)
        P.ins("gpsimd", lambda e: e.memset(upc[:, 0:2], 0.0), pwrites=[B["upc"]], dur=100)
        P.ins("gpsimd", lambda e: e.memset(upc[:, 258:264], 0.0), pwrites=[B["upc"]], dur=100)
        wave = [0]

        YF0 = Yb.bitcast(F32)
        SQ = [YF0[:, 0:2048], YF0[:, 2048:4096]]
        SQB = [Buf("sq0"), Buf("sq1")]
        R3F = ABF[:, R3 + 4096:R3 + 16384].bitcast(F32)
        WAB = [Buf("wa0"), Buf("wa1")]

        def stage_load(src_ap, ncols, nk=8, extra=()):
            w = wave[0]
            wave[0] += 1
            q = SQ[w % 2]
            qB = SQB[w % 2]
            ws3 = q[:, 0:nk * ncols].rearrange("p (k n) -> p k n", k=nk)
            P.dma("w%d" % (w % 2), ws3, src_ap, writes=[qB], nbytes=128 * nk * ncols * 4, extra=extra)
            return ws3, qB

        P.ins("scalar", lambda e: e.activation(out=sc4[:, 0:16], in_=cv4[:, 0:16], func=AF.Silu), reads=[B["s2in"]],
              writes=[B["modT"]], dur=300, tset=T_SILU)
        sc2v = sc4[:, 0:16].rearrange("p (k v) -> p k v", v=2)
        for wv in range(2):
            wsl = R3F[:, wv * 3072:(wv + 1) * 3072].rearrange("p (k n) -> p k n", k=8)
            wB = [WAB[wv]]
            P.dma("wa%d" % wv, wsl, wada_s[:, wv * 384:(wv + 1) * 384].rearrange("(k p) n -> p k n", p=128), writes=wB,
                  nbytes=3 << 19)
            for j3 in range(3):
                jj = wv * 3 + j3
                mm_chain_shared(PB[7], [(lambda e, jj=jj, j3=j3, kc=kc, wsl=wsl: e.matmul(
                    pb[7][:, 2 * jj:2 * jj + 2], lhsT=wsl[:, kc, j3 * 128:(j3 + 1) * 128], rhs=sc2v[:, kc, :],
                    start=(kc == 0), stop=(kc == 7))) for kc in range(8)], reads=wB + [B["modT"]], dur=230)
        P.ins("vector", lambda e: e.memset(modS, 0.0), writes=[B["modS"]], dur=100)
        P.ins("vector", lambda e: e.tensor_tensor(out=modS[:, 0:12].rearrange("p (v j) -> p j v", v=2),
                                                  in0=pb[7][:, 0:12].rearrange("p (j v) -> p j v", v=2),
                                                  in1=badas.rearrange("p (j v) -> p j v", v=2), op=ALU.add),
              reads=[PB[7], B["s2in"]], writes=[B["modS"]], dur=200)
        d_ = P.dma("cmi", cm_in[:, :], modS, reads=[B["modS"]], nbytes=8192)
        ag0 = P.cc("ag", lambda e: e.collective_compute("AllGather", ALU.bypass, replica_groups=[[0, 1, 2, 3], [4, 5, 6, 7]],
                                                         ins=[cm_in], outs=[cm_out]), extra=[d_], dur=38000.0)
        cmv = cm_out[:, 0:12].rearrange("(r p) (v j) -> p r v j", p=128, v=2)
        P.dma("cmx", modX.rearrange("p (r v j) -> p r v j", r=4, v=1), cmv[:, :, 0:1, :], extra=[ag0], writes=[B["modX"]],
              nbytes=16384)
        P.dma("cmc", modC.rearrange("p (r v j) -> p r v j", r=4, v=1), cmv[:, :, 1:2, :], extra=[ag0], writes=[B["modC"]],
              nbytes=16384)
        P.ins("vector", lambda e: e.scalar_tensor_tensor(out=Gx, in0=modX[:, 8:16], scalar=1.0, in1=gain_sb,
                                                         op0=ALU.add, op1=ALU.mult),
              reads=[B["modX"], B["small"]], pwrites=[B["modT"]], dur=200)
        P.ins("vector", lambda e: e.scalar_tensor_tensor(out=Gc, in0=modC[:, 8:16], scalar=1.0, in1=gain_sb,
                                                         op0=ALU.add, op1=ALU.mult),
              reads=[B["modC"], B["small"]], pwrites=[B["modT"]], dur=200)
        sh2v = sh2.rearrange("p (k s) -> p k s", s=2)
        P.ins("vector", lambda e: e.tensor_copy(out=sh2v[:, :, 0], in_=modX[:, 0:8]), reads=[B["modX"]],
              pwrites=[B["sh2"]], dur=150)
        P.ins("vector", lambda e: e.tensor_copy(out=sh2v[:, :, 1], in_=modC[:, 0:8]), reads=[B["modC"]],
              pwrites=[B["sh2"]], dur=150)
        col_ranges = [(0, 256), (256, 512), (512, 768), (768, 896)]
        for wi, (c0, c1) in enumerate(col_ranges):
            wc = c1 - c0
            ws3, qB = stage_load(w_in_c[:, c0:c1].rearrange("(k p) n -> p k n", p=128), wc)
            for kc in range(8):
                P.ins("vector", lambda e, kc=kc, ws3=ws3, c0=c0, c1=c1: e.tensor_scalar(
                    out=Wp[:, kc, c0:c1], in0=ws3[:, kc, :], scalar1=Gx[:, kc:kc + 1], scalar2=None, op0=ALU.mult),
                    reads=[qB, B["modT"]], pwrites=[B["Wp"]], dur=dve_ns(wc))
                if c0 == 0:
                    P.ins("gpsimd", lambda e, kc=kc, ws3=ws3: e.tensor_scalar(
                        out=Wpc[:, kc, :], in0=ws3[:, kc, 128:256], scalar1=Gc[:, kc:kc + 1], scalar2=None,
                        op0=ALU.mult), reads=[qB, B["modT"]], pwrites=[B["Wpc"]], dur=pool_ns(128))
            for cc_ in range(wc // 128):
                ch = c0 // 128 + cc_
                mm_chain_shared(PB[4], [
                    (lambda e, kc=kc, ws3=ws3, cc_=cc_, ch=ch: e.matmul(
                        pb[4][:, 2 * ch:2 * ch + 2], lhsT=ws3[:, kc, cc_ * 128:(cc_ + 1) * 128],
                        rhs=sh2v[:, kc, :], start=(kc == 0), stop=(kc == 7))) for kc in range(8)],
                    reads=[qB, B["sh2"]], dur=230)
        P.ins("vector", lambda e: e.tensor_copy(out=bias_sb, in_=pb[4][:, 0:14]), reads=[PB[4]], writes=[B["bias"]],
              dur=200)

        P.ins("vector", lambda e: e.tensor_copy(out=identf[:], in_=ident[:]), reads=[B["consts"]], writes=[B["identf"]],
              dur=dve_ns(128))
        for h in range(2):
            mm_chain_shared(PB[5 + h], [(lambda e, h=h, jq=jq: e.matmul(
                pb[5 + h][0:1, jq * 128:(jq + 1) * 128], lhsT=modX[:, 16 + 4 * h + jq:17 + 4 * h + jq], rhs=identf[:],
                start=True, stop=True)) for jq in range(4)], reads=[B["modX"], B["identf"]], dur=pe_ns(128, 4))
            P.ins("vector", lambda e, h=h: e.tensor_copy(out=rows[:, h * 512:(h + 1) * 512], in_=pb[5 + h][0:1, :]),
                  reads=[PB[5 + h]], pwrites=[B["rows"]], dur=700)
        for q_ in range(4):
            bk = 5 + (q_ % 2)
            P.ins("tensor", lambda e, q_=q_, bk=bk: e.matmul(pb[bk][:, :], lhsT=ones[0:1, :],
                                                             rhs=rows[0:1, q_ * 512:(q_ + 1) * 512], start=True, stop=True),
                  reads=[B["ones"], B["rows"]], writes=[PB[bk]], dur=pe_ns(512, 4))
            P.ins("scalar", lambda e, q_=q_, bk=bk: e.copy(out=bc[:, q_ * 512:(q_ + 1) * 512], in_=pb[bk][:, :]),
                  reads=[PB[bk]], pwrites=[B["bc"]], dur=act_ns(512))

        for k in range(4):
            P.ins("vector", lambda e, k=k: e.tensor_scalar(out=dconv[:, k * 128:(k + 1) * 128], in0=ident[:],
                                                           scalar1=lv[:, k:k + 1], scalar2=None, op0=ALU.mult),
                  reads=[B["consts"], B["small"]], pwrites=[B["dconv"]], dur=dve_ns(128))
        ws3, qB = stage_load(w_gate[:, :].rearrange("p (k n) -> p k n", k=1), 512, nk=1)
        P.ins("vector", lambda e, ws3=ws3: e.tensor_copy(out=wgate[:], in_=ws3[:, 0, :]), reads=[qB],
              writes=[B["wgate"]], dur=dve_ns(512))
        lam = lv[:, 9:11]
        t_na, t_z, t_d, t_s, t_q, t_p, t_m = [sp_t[:, 2 * i:2 * i + 2] for i in range(7)]
        V = lambda fn: P.ins("vector", fn, reads=[B["small"]], writes=[B["lruc"]], dur=150)
        V(lambda e: e.tensor_scalar(out=t_m, in0=lam, scalar1=-1.0, scalar2=None, op0=ALU.mult))
        V(lambda e: e.tensor_tensor(out=t_na, in0=lam, in1=t_m, op=ALU.min))
        P.ins("scalar", lambda e: e.activation(out=t_z, in_=t_na, func=AF.Exp), reads=[B["lruc"]], writes=[B["lruc"]],
              dur=300, tset=T_EXP)
        V(lambda e: e.tensor_scalar(out=t_d, in0=t_z, scalar1=2.0, scalar2=None, op0=ALU.add))
        V(lambda e: e.reciprocal(out=t_d, in_=t_d))
        V(lambda e: e.tensor_tensor(out=t_s, in0=t_z, in1=t_d, op=ALU.mult))
        V(lambda e: e.tensor_tensor(out=t_q, in0=t_s, in1=t_s, op=ALU.mult))
        V(lambda e: e.tensor_scalar(out=t_p, in0=t_q, scalar1=1.0 / 11.0, scalar2=None, op0=ALU.mult))
        for cst in [1.0 / 9, 1.0 / 7, 1.0 / 5, 1.0 / 3]:
            V(lambda e, cst=cst: e.scalar_tensor_tensor(out=t_p, in0=t_p, scalar=cst, in1=t_q, op0=ALU.add, op1=ALU.mult))
        V(lambda e: e.scalar_tensor_tensor(out=t_p, in0=t_p, scalar=1.0, in1=t_s, op0=ALU.add, op1=ALU.mult))
        V(lambda e: e.tensor_scalar(out=t_m, in0=lam, scalar1=-1.0, scalar2=0.0, op0=ALU.mult, op1=ALU.max))
        V(lambda e: e.tensor_scalar(out=t_m, in0=t_m, scalar1=-4.0, scalar2=None, op0=ALU.mult))
        V(lambda e: e.scalar_tensor_tensor(out=negKh, in0=t_p, scalar=-8.0, in1=t_m, op0=ALU.mult, op1=ALU.add))
        V(lambda e: e.tensor_scalar(out=negK, in0=negKh, scalar1=2.0, scalar2=None, op0=ALU.mult))
        V(lambda e: e.tensor_scalar(out=hbr, in0=lv[:, 5:7], scalar1=0.5, scalar2=None, op0=ALU.mult))
        V(lambda e: e.tensor_scalar(out=hbi, in0=lv[:, 7:9], scalar1=0.5, scalar2=None, op0=ALU.mult))

        checkpoint(0, [("small", small[:], [128, 256], F32), ("wp", ABF[:, R1:R1 + 8192], [128, 8192], BF16),
                       ("bc", bc[:], [128, 2 * D], F32), ("dconv", dconv[:], [128, 512], BF16),
                       ("wgate", wgate[:], [128, 512], BF16)])

        pTbank = [0, 1, 5, 6]
        pT = [pb[i][:].bitcast(BF16) for i in pTbank]
        accbank = [2, 3, 4, 7]
        st1 = {"grp": 0, "tile": 0, "acc": 0, "pacc": 0}
        xb_t = xb.rearrange("(t p) f -> p t f", p=128)

        def dreg(e, eng, name):
            key = eng + name
            if key not in state:
                g_ = e.partition_id() % 4
                expr = {"g": g_, "pr": g_ // 2, "npr": 1 - g_ // 2, "lo": g_ % 2}[name]
                state[key] = e.snap(expr, min_val=0, max_val=3 if name == "g" else 1)
            return state[key]
        xb4 = xb.rearrange("(o t p) f -> p o t f", o=4, p=128)

        def proj_pass(src_fn, ntok, Wsrc, WB, specs, dyn_src=None):
            ngroups = ntok // 256
            n_mm = 512 if ntok >= 512 else ntok
            for g in range(ngroups):
                gi = st1["grp"]
                st1["grp"] += 1
                xs = Q[gi % 4].rearrange("p (t f) -> p t f", t=2)
                xsB = QB[gi % 4]
                if dyn_src is not None:
                    P.dma("x%d" % (gi % 4), None, None, writes=[xsB], nbytes=1 << 20,
                          dyn=lambda e, g=g, xs=xs: (xs.rearrange("p (o t) f -> p o t f", o=1), dyn_src(e, g)))
                else:
                    P.dma("x%d" % (gi % 4), xs, src_fn(g), writes=[xsB], nbytes=1 << 20)
                sq = ssq[gi % 4]
                sqB = B["ssq%d" % (gi % 4)]
                rs = rstd[gi % 4]
                rsB = B["rstd%d" % (gi % 4)]
                P.begin(sqB)
                for t in range(2):
                    jk, jkB = next_junk()
                    P.ins("scalar", lambda e, xs=xs, t=t, sq=sq, jk=jk: e.activation(out=jk, in_=xs[:, t, :], func=AF.Square,
                                                                                   accum_out=sq[:, t:t + 1]),
                          reads=[xsB], writes=[jkB], pwrites=[sqB], dur=act_ns(1024) + 100)
                P.ins("gpsimd", lambda e, sq=sq, rs=rs: e.tensor_scalar(out=rs, in0=sq, scalar1=1.0 / D, scalar2=EPS,
                                                                      op0=ALU.mult, op1=ALU.add),
                      reads=[sqB], writes=[rsB], dur=200)
                P.ins("gpsimd", lambda e, rs=rs: e.tensor_tensor(out=rs, in0=rs, in1=mhalf2, op=ALU.pow),
                      reads=[B["mh"]], writes=[rsB], dur=650)
                for t in range(2):
                    ti = st1["tile"]
                    st1["tile"] += 1
                    xnb = xn[ti % 4]
                    xnB = B["xn%d" % (ti % 4)]
                    P.ins("vector", lambda e, xs=xs, t=t, xnb=xnb, rs=rs: e.tensor_scalar(
                        out=xnb, in0=xs[:, t, :], scalar1=rs[:, t:t + 1], scalar2=None, op0=ALU.mult),
                        reads=[xsB, rsB], writes=[xnB], dur=dve_ns(1024, 0.56))
                    pTb = pT[ti % 4]
                    pTB = PB[pTbank[ti % 4]]
                    mm_group(pTB, [(lambda e, kc=kc, xnb=xnb, pTb=pTb: e.transpose(
                        out=pTb[:, kc * 128:(kc + 1) * 128], in_=xnb[:, kc * 128:(kc + 1) * 128], identity=ident[:]))
                        for kc in range(8)], reads=[xnB, B["consts"]], dur=pe_ns(128))
                    tok_in_pass = (g * 2 + t) * 128
                    ai = st1["acc"]
                    xTb = xT[ai % 2]
                    xTB = B["xT%d" % (ai % 2)]
                    off = tok_in_pass % n_mm
                    if off == 0:
                        P.begin(xTB)
                    pT3 = pTb.rearrange("p (k n) -> p k n", k=8)
                    if False:
                        pass
                    else:
                        P.ins("vector", lambda e, pT3=pT3, xTb=xTb, off=off: e.tensor_copy(
                            out=xTb[:, :, off:off + 128], in_=pT3), reads=[pTB], pwrites=[xTB], dur=dve_ns(1024, 0.9))
                    if off + 128 == n_mm:
                        t0 = tok_in_pass + 128 - n_mm
                        for col0, evac in specs:
                            bk = accbank[st1["pacc"] % 4]
                            st1["pacc"] += 1
                            mm_group(PB[bk], [(lambda e, kc=kc, xTb=xTb, col0=col0, bk=bk, n_mm=n_mm: e.matmul(
                                pb[bk][:, 0:n_mm], lhsT=Wsrc[:, kc, col0:col0 + 128], rhs=xTb[:, kc, 0:n_mm],
                                start=(kc == 0), stop=(kc == 7))) for kc in range(8)],
                                reads=[WB, xTB], dur=pe_ns(n_mm))
                            evac(pb[bk][:, 0:n_mm], PB[bk], t0, n_mm)
                        st1["acc"] += 1

        def evac_act(dst_fn, dstB_fn, func, bias_col, tset):
            def f(ps, psB, t0, n):
                P.ins("scalar", lambda e: e.activation(out=dst_fn(t0, n), in_=ps, func=func, bias=bias_col),
                      reads=[psB, B["bias"]], pwrites=[dstB_fn(t0)], dur=act_ns(n), tset=tset)
            return f

        proj_pass(lambda g: ctxb[g * 256:(g + 1) * 256, :].rearrange("(t p) f -> p t f", p=128), CTX, Wpc, B["Wpc"],
                  [(0, evac_act(lambda t0, n: upc[:, 2 + t0:2 + t0 + n], lambda t0: B["upc"], AF.Identity,
                                bias_sb[:, 3:4], T_ANY))])
        proj_pass(lambda g: xb[g * 256:(g + 1) * 256, :].rearrange("(t p) f -> p t f", p=128), SEQ, Wp, B["Wp"],
                  [(0, evac_act(lambda t0, n: ufT[:, t0:t0 + n], lambda t0: B["ufT"], AF.Identity, bias_sb[:, 0:1], T_ANY)),
                   (128, evac_act(lambda t0, n: upad[:, 2 + t0:2 + t0 + n], lambda t0: UP[t0 // 512], AF.Identity,
                                  bias_sb[:, 2:3], T_ANY)),
                   (256, evac_act(lambda t0, n: sgl[:, t0:t0 + n], lambda t0: SG[t0 // 1024], AF.Silu, bias_sb[:, 4:5],
                                  T_SILU))])

        def own_src(e, g):
            return xb4[:, bass.ds(dreg(e, "sync", "g"), 1), 2 * g:2 * g + 2, :]
        proj_pass(None, OWN, Wp, B["Wp"],
                  [(384 + 128 * fc, evac_act(lambda t0, n, fc=fc: sgf[:, fc * 2048 + t0:fc * 2048 + t0 + n],
                                             lambda t0: B["sgf"], AF.Silu, bias_sb[:, 6 + 2 * fc:7 + 2 * fc], T_SILU))
                   for fc in range(4)], dyn_src=own_src)

        checkpoint(1, [("uft", ufT, [128, 8192], BF16), ("upad", upad, [128, 8200], BF16),
                       ("sgl", sgl, [128, 8192], BF16), ("sgf", sgf, [128, 8192], BF16),
                       ("upc", upc[:], [128, 264], BF16)])

        Z3 = Z0.rearrange("p (j t) -> p j t", t=128)
        Yfull = ABF[:, R3 + 8192:R3 + 24576]
        Y3 = Yfull.rearrange("p (j c) -> p j c", c=128)
        mixv = mixT.rearrange("p (k2 k1) -> p k1 k2", k1=64)
        fb = [2, 3, 4, 7, 0, 1, 5, 6]
        ev = [0]

        def nbank():
            b_ = fb[ev[0] % 8]
            ev[0] += 1
            return b_

        def evac_copy(out_ap, in_ap, psB, n, **kw):
            if ev[0] % 2 == 0:
                return P.ins("scalar", lambda e: e.copy(out=out_ap, in_=in_ap), reads=[psB], dur=act_ns(n), **kw)
            return P.ins("vector", lambda e: e.tensor_copy(out=out_ap, in_=in_ap), reads=[psB], dur=dve_ns(n, 0.9), **kw)

        r1_done = []
        for n_ in ["Wp", "Wpc", "xn0", "xn1", "xn2", "xn3", "xT0", "xT1"]:
            r1_done += B[n_].writers + B[n_].readers + B[n_].war
        for q_ in range(4):
            P.dma("g%d" % q_, gtab[:, q_ * 4096:(q_ + 1) * 4096], gtab_d[:, q_ * 4096:(q_ + 1) * 4096],
                  pwrites=[B["gtab"]], extra=r1_done, nbytes=1 << 20)
        RG = [[0, 1, 2, 3], [4, 5, 6, 7]]
        ag_m, ag_l = [], []
        r3_pre = B["rows"].writers + B["rows"].readers
        for b_ in WAB:
            r3_pre += b_.writers + b_.readers
        p0_stage = []
        for b_ in SQB:
            p0_stage += b_.writers + b_.readers
        for tg in range(32):
            bk = nbank()
            steps = []
            for tt in range(4):
                t2 = tg * 4 + tt
                for ri in range(2):
                    steps.append(lambda e, t2=t2, tt=tt, ri=ri, bk=bk: e.matmul(
                        pb[bk][64 * ri:64 * ri + 64, tt * 128:(tt + 1) * 128], lhsT=ufT[:, t2::128],
                        rhs=f128[:, ri * 128:(ri + 1) * 128], start=True, stop=True))
            mm_group(PB[bk], steps, reads=[B["ufT"], B["consts"]], dur=pe_ns(128))
            evac_copy(Z3[:, :, tg * 4:(tg + 1) * 4], pb[bk][:, :].rearrange("p (t j) -> p j t", t=4), PB[bk], 512,
                      pwrites=[B["Z0"]], extra=r3_pre)
        mix_parts = [[], []]
        P.begin(B["Y"])
        for half in (1, 0):
            y_extra = p0_stage if half == 1 else (list(r3_pre) + B["Z0"].writers + B["Z0"].readers)
            for jg in range(16):
                bk = nbank()
                mm_group(PB[bk], [(lambda e, j=half * 64 + jg * 4 + jj, jj=jj, bk=bk: e.matmul(
                    pb[bk][:, jj * 128:(jj + 1) * 128], lhsT=Z3[:, j, :], rhs=f64[:, :], start=True, stop=True))
                    for jj in range(4)], reads=[B["Z0"], B["consts"]], dur=pe_ns(128))
                evac_copy(Yfull[:, half * 8192 + jg * 512:half * 8192 + (jg + 1) * 512], pb[bk][:, :], PB[bk], 512,
                          pwrites=[B["Y"]], extra=y_extra)
        for kq in range(16):
            bk = nbank()
            steps = []
            for kk in range(4):
                k1 = kq * 4 + kk
                for ri in range(2):
                    steps.append(lambda e, k1=k1, kk=kk, ri=ri, bk=bk: e.matmul(
                        pb[bk][:, kk * 128:(kk + 1) * 128], lhsT=Y3[:, :, ri * 64 + k1],
                        rhs=gtab[:, ri * SEQ + k1 * 128:ri * SEQ + (k1 + 1) * 128], start=(ri == 0), stop=(ri == 1)))
            mm_group(PB[bk], steps, reads=[B["Y"], B["gtab"]], dur=pe_ns(128))
            srcv = pb[bk][:, :].rearrange("p (k1 k2) -> p k1 k2", k1=4)
            uf_war = B["ufT"].writers + B["ufT"].readers
            mix_parts[0].append(evac_copy(mixv[:, kq * 4:(kq + 1) * 4, :], srcv, PB[bk], 512, extra=uf_war))
        checkpoint(3, [("mixed", mixT, [128, 8192], BF16)])
        PAIRS = [[0, 1], [2, 3], [4, 5], [6, 7]]
        mix_all_parts = mix_parts[0] + mix_parts[1]
        ccm_ids = [P.dma("cpm", None, None, extra=mix_all_parts, nbytes=1 << 20,
                         dyn=lambda e: (cpm_in.rearrange("p (a c) -> p a c", a=1),
                                        mixT.rearrange("p (a c) -> p a c", a=2)[:, bass.ds(dreg(e, "sync", "pr"), 1), :]))]
        agm_pair = P.cc("ag", lambda e: e.collective_compute(
            "AllGather", ALU.bypass, replica_groups=PAIRS, ins=[cpm_in], outs=[cpm_out]), extra=[ccm_ids[0]], dur=20000.0)
        agm_oth = []
        mix5 = mixT.rearrange("p (a h k c) -> p a h k c", a=2, h=2, k=2)
        for k in range(2):
            ccm_ids.append(P.dma("c4mi%d" % k, None, None, extra=mix_all_parts, nbytes=1 << 19,
                                 dyn=lambda e, k=k: (c4m_in[k].rearrange("(h p) (a c) -> p a h c", p=128, a=1),
                                                     mix5[:, bass.ds(dreg(e, "sync", "npr"), 1), :, k, :])))
            agm_oth.append(P.cc("ag", lambda e, k=k: e.collective_compute(
                "AllGather", ALU.bypass, replica_groups=RG, ins=[c4m_in[k]], outs=[c4m_out[k]]), extra=[ccm_ids[-1]], dur=32000.0))

        r3_done = list(r3_pre) + B["Z0"].writers + B["Z0"].readers + B["Z0"].war
        lru_first = len(P.nodes)
        LBS = [[Buf("lt%d_%d" % (s_, i)) for i in range(4)] for s_ in range(4)]
        bsets = [[2, 3, 4], [7, 0, 1], [5, 6, 2], [3, 4, 7], [0, 1, 5], [6, 2, 3], [4, 7, 0], [1, 5, 6]]
        lst = {"bank": 0, "cslot": 0}

        def lru_step(up, upB_fn, T, SC, d, ci, prev, set_i, outs):
            s0 = ci * SC
            sub = min(SC, 512)
            nsub = SC // sub
            base = set_i * 2048
            v32 = AF32[:, base:base + SC]
            tr = AF32[:, base + 512:base + 512 + SC]
            tiw = AF32[:, base + 1024:base + 1024 + SC]
            a2 = AF32[:, base + 1536:base + 1536 + SC]
            LB = LBS[set_i]
            for b_ in LB:
                P.begin(b_)
            qwar = []
            for q_ in (QB[set_i],):
                qwar += q_.writers + q_.readers + q_.war
            for sbi in range(nsub):
                t0 = s0 + sbi * sub
                o = sbi * sub
                vslot = set_i
                vb = vbfs[:, vslot * 512:vslot * 512 + sub]
                vbB = B["vbf%d" % vslot]
                bkc, bkr, bki = bsets[lst["bank"] % 8]
                lst["bank"] += 1
                ups = list(dict.fromkeys([upB_fn(max(t0 - 2, 0)), upB_fn(t0), upB_fn(t0 + sub - 1),
                                          upB_fn(min(t0 + sub + 1, T - 1))]))
                mm_group(PB[bkc], [(lambda e, k=k, t0=t0, bkc=bkc, sub=sub: e.matmul(
                    pb[bkc][:, 0:sub], lhsT=dconv[:, k * 128:(k + 1) * 128], rhs=up[:, t0 + k:t0 + k + sub],
                    start=(k == 0), stop=(k == 3))) for k in range(4)],
                    reads=[B["dconv"]] + ups, dur=pe_ns(sub))
                P.ins("vector", lambda e, bkc=bkc, sub=sub, vb=vb: e.tensor_scalar(
                    out=vb, in0=pb[bkc][:, 0:sub], scalar1=lv[:, 4:5], scalar2=None, op0=ALU.add),
                    reads=[PB[bkc], B["small"]], writes=[vbB], dur=dve_ns(sub, 1.0))
                P.ins("tensor", lambda e, vb=vb, bkr=bkr, sub=sub: e.matmul(
                    pb[bkr][:, 0:sub], lhsT=wgate[:, (2 * d) * 128:(2 * d + 1) * 128], rhs=vb, start=True, stop=True),
                    reads=[B["wgate"], vbB], writes=[PB[bkr]], dur=pe_ns(sub))
                P.ins("tensor", lambda e, vb=vb, bki=bki, sub=sub: e.matmul(
                    pb[bki][:, 0:sub], lhsT=wgate[:, (2 * d + 1) * 128:(2 * d + 2) * 128], rhs=vb, start=True, stop=True),
                    reads=[B["wgate"], vbB], writes=[PB[bki]], dur=pe_ns(sub))
                tk_tr = P.ins("scalar", lambda e, o=o, bkr=bkr, sub=sub, tr=tr: e.activation(
                    out=tr[:, o:o + sub], in_=pb[bkr][:, 0:sub], func=AF.Tanh, bias=hbr[:, d:d + 1], scale=0.5),
                    reads=[PB[bkr], B["lruc"]], extra=LB[1].war + qwar, dur=act_ns(sub), tset=T_EXP)
                P.ins("scalar", lambda e, o=o, bki=bki, sub=sub, tiw=tiw: e.activation(
                    out=tiw[:, o:o + sub], in_=pb[bki][:, 0:sub], func=AF.Tanh, bias=hbi[:, d:d + 1], scale=0.5),
                    reads=[PB[bki], B["lruc"]], pwrites=[LB[2]], extra=qwar, dur=act_ns(sub), tset=T_EXP)
                tk_a2 = P.ins("scalar", lambda e, o=o, sub=sub, tr=tr, a2=a2: e.activation(
                    out=a2[:, o:o + sub], in_=tr[:, o:o + sub], func=AF.Exp, bias=negK[:, d:d + 1], scale=negK[:, d:d + 1]),
                    reads=[B["lruc"]], pwrites=[LB[3]], extra=[tk_tr] + qwar, dur=act_ns(sub), tset=T_EXP)
                P.ins("scalar", lambda e, o=o, sub=sub, tr=tr: e.activation(
                    out=tr[:, o:o + sub], in_=tr[:, o:o + sub], func=AF.Exp, bias=negKh[:, d:d + 1], scale=negKh[:, d:d + 1]),
                    reads=[B["lruc"]], pwrites=[LB[1]], extra=[tk_tr, tk_a2], dur=act_ns(sub), tset=T_EXP)
                tk_w = P.ins("vector", lambda e, o=o, sub=sub, tiw=tiw, vb=vb: e.scalar_tensor_tensor(
                    out=tiw[:, o:o + sub], in0=tiw[:, o:o + sub], scalar=1.0, in1=vb,
                    op0=ALU.add, op1=ALU.mult), reads=[LB[2], vbB], dur=dve_ns(sub, 1.0))
                LB[2].writers.append(tk_w)

            def post():
                P.ins("scalar", lambda e, a2=a2: e.activation(out=a2, in_=a2, func=AF.Sqrt, bias=0.25, scale=-0.25),
                      reads=[LB[3]], writes=[LB[3]], dur=act_ns(SC), tset=T_SQRT)
                P.ins("vector", lambda e, tiw=tiw, a2=a2: e.tensor_tensor(out=tiw, in0=tiw, in1=a2, op=ALU.mult),
                      reads=[LB[3]], writes=[LB[2]], dur=dve_ns(SC, 1.7))
                prev_ap, prevB = prev()
                if d == 1:
                    P.ins("vector", lambda e, v32=v32, tr=tr, tiw=tiw, prev_ap=prev_ap: e.tensor_tensor_scan(
                        out=v32[:, ::-1], data0=tr[:, ::-1], data1=tiw[:, ::-1], initial=prev_ap, op0=ALU.mult, op1=ALU.add),
                        reads=[prevB, LB[1], LB[2]], writes=[LB[0]], extra=qwar, dur=dve_ns(SC, 1.7))
                else:
                    P.ins("vector", lambda e, v32=v32, tr=tr, tiw=tiw, prev_ap=prev_ap: e.tensor_tensor_scan(
                        out=v32, data0=tr, data1=tiw, initial=prev_ap, op0=ALU.mult, op1=ALU.add),
                        reads=[prevB, LB[1], LB[2]], writes=[LB[0]], extra=qwar, dur=dve_ns(SC, 1.7))
                cidx = lst["cslot"] % 8
                lst["cslot"] += 1
                csrc = v32[:, 0:1] if d == 1 else v32[:, SC - 1:SC]
                P.ins("vector", lambda e, cidx=cidx, csrc=csrc: e.tensor_copy(out=carry[:, cidx:cidx + 1], in_=csrc),
                      reads=[LB[0]], writes=[CAR[cidx]], dur=150)
                outs(v32, LB[0], s0, SC)
                return (carry[:, cidx:cidx + 1], CAR[cidx])
            return post

        def lru_all(up, upB_fn, T, SC, init_f, init_b):
            nsc = T // SC
            cur = {"f": init_f, "b": init_b}
            for i in range(nsc):
                first = (2 * i < nsc)

                def outs(v32, vB, s0, n, first=first):
                    if T == CTX:
                        return
                    c_ = s0 // SC
                    if first:
                        P.ins("scalar", lambda e: e.copy(out=hst[:, s0:s0 + n], in_=v32), reads=[vB],
                              writes=[HS[c_]], extra=r3_done, dur=act_ns(n))
                    else:
                        P.ins("gpsimd", lambda e: e.tensor_tensor(out=v32, in0=v32, in1=hst[:, s0:s0 + n], op=ALU.add),
                              reads=[HS[c_]], writes=[vB], dur=pool_ns(n, 2.0))
                        P.ins("vector", lambda e: e.tensor_tensor(out=sgl[:, s0:s0 + n], in0=v32, in1=sgl[:, s0:s0 + n],
                                                                  op=ALU.mult), reads=[vB], writes=[SG[s0 // 1024]],
                              dur=dve_ns(n, 1.7))
                post_f = lru_step(up, upB_fn, T, SC, 0, i, lambda: cur["f"], i % 2, outs)
                post_b = lru_step(up, upB_fn, T, SC, 1, nsc - 1 - i, lambda: cur["b"], 2 + i % 2, outs)
                cur["f"] = post_f()
                cur["b"] = post_b()
            return cur["f"], cur["b"]

        cf_, cb_ = lru_all(upc, lambda t: B["upc"], CTX, CTX, (zero1, B["mh"]), (zero1, B["mh"]))
        P.ins("vector", lambda e: e.tensor_copy(out=h0[:, 0:1], in_=cf_[0]), reads=[cf_[1]], pwrites=[B["h0"]], dur=150)
        P.ins("vector", lambda e: e.tensor_copy(out=h0[:, 1:2], in_=cb_[0]), reads=[cb_[1]], pwrites=[B["h0"]], dur=150)
        lru_all(upad, lambda t: UP[t // 512], SEQ, 512, (h0[:, 0:1], B["h0"]), (h0[:, 1:2], B["h0"]))
        lru_nodes = list(range(lru_first, len(P.nodes)))
        checkpoint(2, [("ylg", sgl, [128, 8192], BF16), ("small", small[:], [128, 256], F32)])
        sg_all = []
        for b_ in SG:
            sg_all += b_.writers
        PAIRS = [[0, 1], [2, 3], [4, 5], [6, 7]]
        d_ = P.dma("cpa", None, None, extra=sg_all, nbytes=1 << 20, eng="scalar",
                   dyn=lambda e: (cpa_in.rearrange("p (a c) -> p a c", a=1),
                                  sgl.rearrange("p (a c) -> p a c", a=2)[:, bass.ds(dreg(e, "scalar", "pr"), 1), :]))
        ag_pair = P.cc("ag", lambda e: e.collective_compute(
            "AllGather", ALU.bypass, replica_groups=PAIRS, ins=[cpa_in], outs=[cpa_out]), extra=[d_], dur=17000.0)
        ag_oth = []
        sgl5 = sgl.rearrange("p (a h k c) -> p a h k c", a=2, h=2, k=2)
        for k in range(2):
            d_ = P.dma("c4i%d" % k, None, None, extra=sg_all, nbytes=1 << 19, eng="scalar",
                       dyn=lambda e, k=k: (c4_in[k].rearrange("(h p) (a c) -> p a h c", p=128, a=1),
                                           sgl5[:, bass.ds(dreg(e, "scalar", "npr"), 1), :, k, :]))
            ag_oth.append(P.cc("ag", lambda e, k=k: e.collective_compute(
                "AllGather", ALU.bypass, replica_groups=RG, ins=[c4_in[k]], outs=[c4_out[k]]), extra=[d_], dur=27000.0))

        g_done = B["gtab"].writers + B["gtab"].readers
        y_done = B["Y"].writers + B["Y"].readers + B["Y"].war
        YF = Yb.bitcast(F32)
        YQ = [YF[:, 0:2048], YF[:, 2048:4096]]
        YQB = [Buf("yq0"), Buf("yq1")]
        wave5 = [0]

        def stage_load(src_ap, ncols, nk=8, extra=()):
            w = wave5[0]
            wave5[0] += 1
            ws3_ = YQ[w % 2][:, 0:nk * ncols].rearrange("p (k n) -> p k n", k=nk)
            P.dma("wy%d" % (w % 2), ws3_, src_ap, writes=[YQB[w % 2]], nbytes=128 * nk * ncols * 4, extra=y_done)
            return ws3_, YQB[w % 2]
        ws3, qB = stage_load(w_four.rearrange("(k p) n -> p k n", p=128), 512, nk=4, extra=lru_nodes)
        P.ins("gpsimd", lambda e, ws3=ws3: e.tensor_copy(out=wfourp, in_=ws3), reads=[qB], writes=[B["wfourp"]],
              extra=g_done, dur=pool_ns(2048, 1.2))
        for wv in range(4):
            ws3, qB = stage_load(w_out[:, wv * 256:(wv + 1) * 256].rearrange("(k p) n -> p k n", p=128), 256,
                                 extra=lru_nodes)
            for fc in range(8):
                P.ins("vector", lambda e, fc=fc, wv=wv, ws3=ws3: e.tensor_tensor(
                    out=woutp[:, fc, wv * 256:(wv + 1) * 256], in0=ws3[:, fc, :], in1=bc[:, wv * 256:(wv + 1) * 256],
                    op=ALU.mult), reads=[qB, B["bc"]], pwrites=[B["woutp"]], extra=g_done, dur=dve_ns(256, 1.7))
        hs_done = list(r3_done)
        for b_ in HS:
            hs_done += b_.writers + b_.readers

        def get_pidown(e):
            if "pido" not in state:
                state["pido"] = e.snap((e.partition_id() % 4) * OWN, min_val=0, max_val=3 * OWN)
            return state["pido"]
        def get_pid512(e):
            if "pid512" not in state:
                state["pid512"] = e.snap((e.partition_id() % 4) * 512, min_val=0, max_val=3 * 512)
            return state["pid512"]
        ga_l = []
        gl_par = P.dma("gal1", None, None, extra=hs_done + [ag_pair], nbytes=1 << 20, eng="scalar",
                       dyn=lambda e: (ylgall[:, 0:2, :].rearrange("p r (l c) -> p r l c", l=1),
                                      cpa_out.rearrange("(r p) (l c) -> p r l c", p=128, l=2)[
                                          :, :, bass.ds(dreg(e, "scalar", "lo"), 1), :]))
        gl_oth = []
        for k in range(2):
            gl_oth.append(P.dma("gal%d" % (2 + k), None, None, extra=hs_done + [ag_oth[k]], nbytes=1 << 19, eng="scalar",
                                dyn=lambda e, k=k: (ylgall[:, 2:4, k * 1024:(k + 1) * 1024].rearrange("p (a r) (h c) -> p a r h c", a=1, h=1),
                                                    c4_out[k].rearrange("(a r h p) c -> p a r h c", a=2, r=2, h=2)[
                                                        :, bass.ds(dreg(e, "scalar", "npr"), 1), :, bass.ds(dreg(e, "scalar", "lo"), 1), :])))
        gm_par = P.dma("gam1", None, None, extra=r3_done + y_done + [agm_pair], nbytes=1 << 20, eng="gpsimd",
                       dyn=lambda e: (mixall[:, 0:2, :].rearrange("p r (l c) -> p r l c", l=1),
                                      cpm_out.rearrange("(r p) (l c) -> p r l c", p=128, l=2)[
                                          :, :, bass.ds(dreg(e, "gpsimd", "lo"), 1), :]))
        gm_oth = []
        for k in range(2):
            gm_oth.append(P.dma("gam%d" % (2 + k), None, None, extra=r3_done + y_done + [agm_oth[k]], nbytes=1 << 19, eng="gpsimd",
                                dyn=lambda e, k=k: (mixall[:, 2:4, k * 1024:(k + 1) * 1024].rearrange("p (a r) (h c) -> p a r h c", a=1, h=1),
                                                    c4m_out[k].rearrange("(a r h p) c -> p a r h c", a=2, r=2, h=2)[
                                                        :, bass.ds(dreg(e, "gpsimd", "npr"), 1), :, bass.ds(dreg(e, "gpsimd", "lo"), 1), :])))
        ga_m = [[gm_par, gm_oth[0]], [gm_par, gm_oth[0]], [gm_par, gm_oth[1]], [gm_par, gm_oth[1]]]
        checkpoint(4, [("mixall", ABF[:, R3 + 8192:R3 + 16384], [128, 8192], BF16),
                       ("ylgall", ABF[:, R3:R3 + 8192], [128, 8192], BF16)])
        yfg_parts = [[] for _ in range(4)]

        def yf_quarter(tcn):
            for fc in range(4):
                bk = nbank()
                mm_group(PB[bk], [(lambda e, g=g, fc=fc, tcn=tcn, bk=bk: e.matmul(
                    pb[bk][:, :], lhsT=wfourp[:, g, fc * 128:(fc + 1) * 128], rhs=mixall[:, g, tcn * 512:(tcn + 1) * 512],
                    start=(g == 0), stop=(g == 3))) for g in range(4)], reads=[B["wfourp"]], first_extra=ga_m[tcn],
                    dur=pe_ns(512))
                yfg_parts[tcn].append(P.ins("vector", lambda e, fc=fc, tcn=tcn, bk=bk: e.tensor_tensor(
                    out=yfg[:, fc, tcn * 512:(tcn + 1) * 512], in0=pb[bk][:, :],
                    in1=sgf[:, fc * 2048 + tcn * 512:fc * 2048 + (tcn + 1) * 512], op=ALU.mult),
                    reads=[PB[bk], B["sgf"]], extra=g_done, dur=dve_ns(512, 1.8)))
        outs_ = []
        ufF = ufT.bitcast(F32)
        rmF = Yb.bitcast(F32)
        upF = ABF[:, 8192:16384].bitcast(F32)
        ccm_done = list(ccm_ids)
        stg_done = []
        for b_ in YQB:
            stg_done += b_.writers + b_.readers
        xo_regions = [(ufF, ccm_done + mix_parts[0] + mix_parts[1]), (rmF, stg_done + y_done),
                      (upF, lru_nodes), (AF32[:, 0:4096], lru_nodes)]
        xo_bufs = []
        XOB = []
        for gi_, (reg_, ex_) in enumerate(xo_regions):
            gB = Buf("xo_g%d" % gi_)
            P.dma("xo%d" % gi_, None, None, writes=[gB], extra=ex_, nbytes=1 << 21,
                  dyn=lambda e, gi_=gi_, reg_=reg_: (reg_.rearrange("p (o t f) -> p o t f", o=1, t=4),
                                                     xb4[:, bass.ds(dreg(e, "sync", "g"), 1), 4 * gi_:4 * gi_ + 4, :]))
            for i in range(4):
                xo_bufs.append((reg_[:, i * 1024:(i + 1) * 1024], ex_))
                XOB.append(gB)
        XP = [Buf("xp%d" % i) for i in range(16)]

        def acc_phase(tt, fcs, extra):
            xob, xo_extra = xo_bufs[tt]
            for half in range(2):
                bk = nbank()
                mm_group(PB[bk], [(lambda e, fc=fc, half=half, bk=bk, tt=tt: e.matmul(
                    pb[bk][:, :], lhsT=(yfg[:, fc, tt * 128:(tt + 1) * 128] if fc < 4 else ylgall[:, fc - 4, tt * 128:(tt + 1) * 128]),
                    rhs=woutp[:, fc, half * 512:(half + 1) * 512], start=(fc == fcs[0]), stop=(fc == fcs[-1]))) for fc in fcs],
                    reads=[B["woutp"]], first_extra=extra, dur=pe_ns(512))
                tk = P.ins("vector", lambda e, half=half, bk=bk, xob=xob: e.tensor_tensor(
                    out=xob[:, half * 512:(half + 1) * 512], in0=pb[bk][:, :], in1=xob[:, half * 512:(half + 1) * 512],
                    op=ALU.add), reads=[PB[bk], XOB[tt], XP[tt]], dur=dve_ns(512, 1.8))
                XP[tt].writers.append(tk)
        for tt in range(16):
            if tt % 4 == 0:
                yf_quarter(tt // 4)
            acc_phase(tt, [0, 1, 2, 3], yfg_parts[tt // 4])
        for tt in range(16):
            acc_phase(tt, [4, 5], [gl_par])
        XS = [Buf("xs5_%d" % i) for i in range(16)]
        XR = [Buf("xr5_%d" % i) for i in range(16)]
        for tt in range(16):
            xob, xo_extra = xo_bufs[tt]
            xpB = XP[tt]
            for half in range(2):
                bk = nbank()
                mm_group(PB[bk], [(lambda e, fc=fc, half=half, bk=bk, tt=tt: e.matmul(
                    pb[bk][:, :], lhsT=ylgall[:, fc - 4, tt * 128:(tt + 1) * 128],
                    rhs=woutp[:, fc, half * 512:(half + 1) * 512], start=(fc == 6), stop=(fc == 7))) for fc in (6, 7)],
                    reads=[B["woutp"]], first_extra=[gl_oth[tt // 8]], dur=pe_ns(512))
                tk = P.ins("vector", lambda e, half=half, bk=bk, xob=xob: e.tensor_tensor(
                    out=xob[:, half * 512:(half + 1) * 512], in0=pb[bk][:, :], in1=xob[:, half * 512:(half + 1) * 512],
                    op=ALU.add), reads=[PB[bk], xpB], dur=dve_ns(512, 1.8))
                xpB.writers.append(tk)
            s5 = small2[:, 160 + tt:161 + tt]
            r5 = small2[:, 176 + tt:177 + tt]
            jk, jkB = next_junk()
            P.ins("scalar", lambda e, xob=xob, s5=s5, jk=jk: e.activation(out=jk, in_=xob, func=AF.Square, accum_out=s5),
                  reads=[xpB], writes=[XS[tt], jkB], dur=act_ns(1024) + 100)
            P.ins("gpsimd", lambda e, s5=s5, r5=r5: e.tensor_scalar(out=r5, in0=s5, scalar1=1.0 / D, scalar2=EPS,
                                                                  op0=ALU.mult, op1=ALU.add), reads=[XS[tt]], writes=[XR[tt]], dur=200)
            P.ins("gpsimd", lambda e, r5=r5: e.tensor_tensor(out=r5, in0=r5, in1=mhalf, op=ALU.pow),
                  reads=[B["mh"]], writes=[XR[tt]], dur=650)
            P.ins("vector", lambda e, xob=xob, r5=r5: e.scalar_tensor_tensor(out=xob, in0=xob, scalar=r5,
                                                                            in1=bc[:, D:2 * D], op0=ALU.mult, op1=ALU.mult),
                  reads=[XR[tt], B["bc"]], writes=[xpB], dur=dve_ns(1024, 1.7))
            outs_.append(P.dma("out%d" % (tt % 4), y[tt * 128:(tt + 1) * 128, :], xob, reads=[xpB], nbytes=1 << 19))
        P.finish(outs_[-4:])
        state["sim_end"] = P.sim_end
        build_program.sim_end = P.sim_end
    except _Stop:
        pass
    return nc


_CACHE = {}


def _consts():
    if "c" in _CACHE:
        return _CACHE["c"]
    bf = ml_dtypes.bfloat16
    ident = np.eye(128, dtype=np.float32).astype(bf)
    c = np.arange(128)[:, None].astype(np.float64)
    j = np.arange(128)[None, :].astype(np.float64)
    C128 = np.cos(2 * np.pi * c * j / 128)
    S128 = np.sin(2 * np.pi * c * j / 128)
    f128 = np.concatenate([C128, -S128], axis=1).astype(np.float32).astype(bf)
    t1 = np.arange(64)[:, None].astype(np.float64)
    k1 = np.arange(64)[None, :].astype(np.float64)
    C64 = np.cos(2 * np.pi * t1 * k1 / 64)
    S64 = np.sin(2 * np.pi * t1 * k1 / 64)
    f64 = np.concatenate([np.concatenate([C64, -S64], axis=1), np.concatenate([S64, C64], axis=1)],
                         axis=0).astype(np.float32).astype(bf)
    t2 = np.arange(128)[:, None, None].astype(np.float64)
    k1 = np.arange(64)[None, :, None].astype(np.float64)
    k2 = np.arange(128)[None, None, :].astype(np.float64)
    ang = 2 * np.pi * ((t2 * (k1 + 64 * k2)) % SEQ) / SEQ
    Gc = (np.cos(ang) / 1024.0).reshape(128, SEQ)
    Gs = (np.sin(ang) / 1024.0).reshape(128, SEQ)
    gtab = np.concatenate([Gc, Gs], axis=1).astype(np.float32).astype(bf)
    _CACHE["c"] = (ident, f128, f64, gtab)
    return _CACHE["c"]


def _prep(x, c, ctx, c_ctx, w_ada, b_ada, norm_gain, w_in, w_four, conv_w, conv_b,
          w_rg, b_rg, w_ig, b_ig, lam, w_out, final_gain):
    f = lambda a: np.ascontiguousarray(np.asarray(a, dtype=np.float32))
    x, c, ctx, c_ctx = f(x), f(c), f(ctx), f(c_ctx)
    w_ada, b_ada, norm_gain, w_in = f(w_ada)[0], f(b_ada)[0], f(norm_gain)[0], f(w_in)[0]
    w_four, conv_w, conv_b = f(w_four)[0], f(conv_w)[0], f(conv_b)[0]
    w_rg, b_rg, w_ig, b_ig, lam = f(w_rg)[0], f(b_rg)[0], f(w_ig)[0], f(b_ig)[0], f(lam)[0]
    w_out, final_gain = f(w_out)[0], f(final_gain)
    ident, f128, f64, gtab = _consts()
    col = lambda v: np.ascontiguousarray(v.reshape(-1, 128).T)
    in_maps = []
    for core in range(8):
        b, g = core // 4, core % 4
        cvec2 = np.zeros((128, 8, 2), np.float32)
        cvec2[:, :, 0] = col(c[b])
        cvec2[:, :, 1] = col(c_ctx)
        cvec2 = np.ascontiguousarray(cvec2.reshape(128, 16))
        wada_s = np.ascontiguousarray(w_ada[:, 768 * g:768 * (g + 1)])
        bada_s = np.ascontiguousarray(np.repeat(col(b_ada[768 * g:768 * (g + 1)]), 2, axis=1))
        sl = slice(g * 128, (g + 1) * 128)
        w_in_c = np.concatenate([w_in[:, sl], w_in[:, 1024 + g * 128:1024 + (g + 1) * 128],
                                 w_in[:, 1536 + g * 128:1536 + (g + 1) * 128], w_in[:, 512:1024]], axis=1)
        lruvec = np.zeros((128, 16), np.float32)
        for k in range(4):
            lruvec[:, k] = conv_w[k, sl]
        lruvec[:, 4] = conv_b[sl]
        lruvec[:, 5] = b_rg[0, sl]
        lruvec[:, 6] = b_rg[1, sl]
        lruvec[:, 7] = b_ig[0, sl]
        lruvec[:, 8] = b_ig[1, sl]
        lruvec[:, 9] = lam[0, sl]
        lruvec[:, 10] = lam[1, sl]
        w_gate = np.concatenate([w_rg[0, g], w_ig[0, g], w_rg[1, g], w_ig[1, g]], axis=1)
        pr = g // 2
        slots = [2 * pr, 2 * pr + 1, 2 * (1 - pr), 2 * (1 - pr) + 1]
        w_out_c = np.concatenate([w_out[0:512]] + [w_out[512 + 128 * s_:512 + 128 * (s_ + 1)] for s_ in slots], axis=0)
        w_four_c = np.concatenate([w_four[128 * s_:128 * (s_ + 1), :] for s_ in slots], axis=0)
        in_maps.append({
            "xb": x[b], "ctxb": ctx[b], "cvec2": cvec2, "wada_s": wada_s, "bada_s": bada_s,
            "gaincol": col(norm_gain),
            "w_in_c": np.ascontiguousarray(w_in_c), "w_four": np.ascontiguousarray(w_four_c), "w_out": np.ascontiguousarray(w_out_c), "lruvec": lruvec,
            "w_gate": np.ascontiguousarray(w_gate), "fgrow": np.ascontiguousarray(final_gain.reshape(1, D)),
            "ident": ident, "f128": f128, "f64": f64, "gtab": gtab,
        })
    return in_maps


def kernel(**inputs):
    in_maps = _prep(**inputs)
    if "nc" not in _CACHE:
        _CACHE["nc"] = build_program()
    nc = _CACHE["nc"]
    res = run_bass_kernel_spmd(nc, in_maps, core_ids=list(range(8)))
    out = np.zeros((2, SEQ, D), np.float32)
    for core in range(8):
        b, g = core // 4, core % 4
        out[b, g * OWN:(g + 1) * OWN, :] = res.results[core]["y"]
    return out
```

```python
import numpy as np
import ml_dtypes
from contextlib import ExitStack
import concourse.bass as bass
import concourse.mybir as mybir
from concourse.bass_utils import run_bass_kernel_spmd

F32 = mybir.dt.float32
BF16 = mybir.dt.bfloat16
AF = mybir.ActivationFunctionType
ALU = mybir.AluOpType

D = 1024
SEQ = 8192
CTX = 256
OWN = 2048
EPS = 1e-6

T_ANY, T_SILU, T_EXP, T_SQRT = 0, 1, 2, 3


class Buf:
    def __init__(self, name):
        self.name = name
        self.writers = []
        self.readers = []
        self.war = []


class Node:
    __slots__ = ("id", "eng", "fn", "deps", "dur", "kind", "key", "nbytes", "tset", "seq", "semval", "start", "end")


def act_ns(n):
    return 220 + 0.78 * n


def dve_ns(n, c=1.0):
    return 150 + c * n


def pool_ns(n, c=1.8):
    return 150 + c * n


def pe_ns(n, f=1):
    return max(120.0, 60 + 0.56 * n) * f


class Prog:
    ENGS = ["sync", "scalar", "vector", "gpsimd", "tensor"]

    def __init__(self, nc, stack):
        self.nc = nc
        self.stack = stack
        self.nodes = []

    def _deps(self, reads, writes, pwrites, extra):
        deps = set(x for x in extra if x is not None)
        for b in reads:
            deps.update(b.writers)
        for b in writes:
            deps.update(b.writers)
            deps.update(b.readers)
            deps.update(b.war)
        for b in pwrites:
            deps.update(b.war)
        return deps

    def _mark(self, nid, reads, writes, pwrites):
        for b in reads:
            b.readers.append(nid)
        for b in writes:
            b.writers = [nid]
            b.readers = []
            b.war = []
        for b in pwrites:
            b.writers.append(nid)

    def begin(self, b):
        b.war = list(b.writers) + list(b.readers) + list(b.war)
        b.writers = []
        b.readers = []

    def _add(self, eng, fn, deps, dur, kind, key=None, nbytes=0, tset=T_ANY):
        n = Node()
        n.id = len(self.nodes)
        n.eng = eng
        n.fn = fn
        n.deps = deps
        n.dur = dur
        n.kind = kind
        n.key = key
        n.nbytes = nbytes
        n.tset = tset
        self.nodes.append(n)
        return n.id

    def ins(self, eng, fn, reads=(), writes=(), pwrites=(), extra=(), dur=300.0, tset=T_ANY):
        nid = self._add(eng, fn, self._deps(reads, writes, pwrites, extra), dur, "ins", tset=tset)
        self._mark(nid, reads, writes, pwrites)
        return nid

    def dma(self, key, out, in_, reads=(), writes=(), pwrites=(), extra=(), nbytes=0, dyn=None, eng="sync"):
        if dyn is None:
            fn = lambda e, out=out, in_=in_: e.dma_start(out=out, in_=in_)
        else:
            def fn(e, dyn=dyn):
                o, i = dyn(e)
                return e.dma_start(out=o, in_=i)
        nid = self._add(eng, fn, self._deps(reads, writes, pwrites, extra), 60.0, "dma", key=key, nbytes=nbytes)
        self._mark(nid, reads, writes, pwrites)
        return nid

    def cc(self, key, fn, reads=(), writes=(), extra=(), dur=50000.0):
        nid = self._add("gpsimd", fn, self._deps(reads, writes, (), extra), dur, "cc", key=key)
        self._mark(nid, reads, writes, ())
        return nid

    def schedule(self):
        nodes = self.nodes
        N = len(nodes)
        succ = [[] for _ in range(N)]
        npred = [0] * N
        for n in nodes:
            npred[n.id] = len(n.deps)
            for d in n.deps:
                succ[d].append(n.id)
        cand = {e: [] for e in self.ENGS}
        ready_t = [0.0] * N
        for n in nodes:
            if npred[n.id] == 0:
                cand[n.eng].append(n.id)
        free_at = {e: 0.0 for e in self.ENGS}
        cur_tset = T_ANY
        dma_pipe = 0.0
        cc_pipe = 0.0
        order = {e: [] for e in self.ENGS}
        done = 0
        BW = 190.0
        while done < N:
            best = None
            for e in self.ENGS:
                cl = cand[e]
                if not cl:
                    continue
                now = free_at[e]
                rdy = [i for i in cl if ready_t[i] <= now]
                if rdy:
                    if e == "scalar":
                        comp = [i for i in rdy if nodes[i].tset == T_ANY or nodes[i].tset == cur_tset]
                        pick = min(comp) if comp else min(rdy)
                    else:
                        pick = min(rdy)
                    st = now
                else:
                    pick = min(cl, key=lambda i: (ready_t[i], i))
                    st = ready_t[pick]
                if best is None or st < best[0] or (st == best[0] and pick < best[2]):
                    best = (st, e, pick)
            st, e, pick = best
            n = nodes[pick]
            cand[e].remove(pick)
            dur = n.dur
            if e == "scalar" and n.tset != T_ANY and n.tset != cur_tset:
                dur += 1300.0
                cur_tset = n.tset
            n.start = st
            if n.kind == "dma":
                free_at[e] = st + dur
                dma_pipe = max(dma_pipe, st) + n.nbytes / BW
                n.end = dma_pipe + 2000.0
            elif n.kind == "cc":
                free_at[e] = st + 100.0
                cc_pipe = max(cc_pipe, st) + dur
                n.end = cc_pipe
            else:
                free_at[e] = st + dur
                n.end = st + dur + 60.0
            order[e].append(pick)
            done += 1
            for s in succ[pick]:
                npred[s] -= 1
                if n.end > ready_t[s]:
                    ready_t[s] = n.end
                if npred[s] == 0:
                    cand[nodes[s].eng].append(s)
        self.sim_end = max(n.end for n in nodes)
        return order

    def finish(self, final_ids):
        nc = self.nc
        order = self.schedule()
        nodes = self.nodes
        sem = {e: self.stack.enter_context(nc.semaphore("done_" + e)) for e in self.ENGS}
        dsem = {}
        dcnt = {}
        for e in self.ENGS:
            k = 0
            for i in order[e]:
                n = nodes[i]
                if n.kind == "ins":
                    k += 1
                    n.seq = k
        for e in self.ENGS:
            for i in order[e]:
                n = nodes[i]
                if n.kind in ("dma", "cc"):
                    if n.key not in dsem:
                        dsem[n.key] = self.stack.enter_context(nc.semaphore("d_" + n.key))
                        dcnt[n.key] = 0
                    dcnt[n.key] += 16 if n.kind == "dma" else 1
                    n.semval = dcnt[n.key]
        waited = {}
        prog = {e: [] for e in self.ENGS}
        for e in self.ENGS:
            for i in order[e]:
                n = nodes[i]
                need = {}
                for d in n.deps:
                    dn = nodes[d]
                    if dn.kind == "ins":
                        if dn.eng == e and e == "tensor":
                            continue
                        s_, v_ = sem[dn.eng], dn.seq
                    else:
                        s_, v_ = dsem[dn.key], dn.semval
                    k_ = id(s_)
                    if k_ not in need or need[k_][1] < v_:
                        need[k_] = (s_, v_)
                waits = []
                for k_, (s_, v_) in need.items():
                    if waited.get((e, k_), 0) >= v_:
                        continue
                    waited[(e, k_)] = v_
                    waits.append((s_, v_))
                prog[e].append((n, waits))
        fin = []
        for i in final_ids:
            dn = nodes[i]
            fin.append((dsem[dn.key], dn.semval))

        def run(e, name):
            for n, waits in prog[name]:
                for s_, v_ in waits:
                    e.wait_ge(s_, v_)
                if n.kind == "ins":
                    n.fn(e).then_inc(sem[name], 1)
                elif n.kind == "dma":
                    n.fn(e).then_inc(dsem[n.key], 16)
                else:
                    n.fn(e).then_inc(dsem[n.key], 1)
            if name == "sync":
                for s_, v_ in fin:
                    e.wait_ge(s_, v_)

        with nc.Block() as block:
            @block.sync
            def _(e):
                run(e, "sync")

            @block.scalar
            def _(e):
                run(e, "scalar")

            @block.vector
            def _(e):
                run(e, "vector")

            @block.gpsimd
            def _(e):
                run(e, "gpsimd")

            @block.tensor
            def _(e):
                run(e, "tensor")


class _Stop(Exception):
    pass


def build_program(stop=99):
    nc = bass.Bass("TRN2", target_bir_lowering=False)
    dti = lambda n, s, d: nc.dram_tensor(n, s, d, kind="ExternalInput").ap()
    xb = dti("xb", [SEQ, D], F32)
    ctxb = dti("ctxb", [CTX, D], F32)
    cvec2 = dti("cvec2", [128, 16], F32)
    wada_s = dti("wada_s", [D, 768], F32)
    bada_s = dti("bada_s", [128, 12], F32)
    cm_in = nc.dram_tensor("cm_in", [128, 16], F32, kind="Internal").ap()
    cm_out = nc.dram_tensor("cm_out", [512, 16], F32, kind="Internal").ap()
    gaincol = dti("gaincol", [128, 8], F32)
    w_in_c = dti("w_in_c", [D, 896], F32)
    w_four = dti("w_four", [512, 512], F32)
    w_out = dti("w_out", [D, D], F32)
    lruvec = dti("lruvec", [128, 16], F32)
    w_gate = dti("w_gate", [128, 512], F32)
    fgrow = dti("fgrow", [1, D], F32)
    ident_d = dti("ident", [128, 128], BF16)
    f128_d = dti("f128", [128, 256], BF16)
    f64_d = dti("f64", [128, 128], BF16)
    gtab_d = dti("gtab", [128, 2 * SEQ], BF16)
    y = nc.dram_tensor("y", [OWN, D], F32, kind="ExternalOutput").ap()
    cpa_in = nc.dram_tensor("cpa_in", [128, 2048], BF16, kind="Internal").ap()
    cpa_out = nc.dram_tensor("cpa_out", [256, 2048], BF16, kind="Internal").ap()
    c4_in = [nc.dram_tensor("c4_in%d" % i, [256, 1024], BF16, kind="Internal").ap() for i in range(2)]
    c4_out = [nc.dram_tensor("c4_out%d" % i, [1024, 1024], BF16, kind="Internal").ap() for i in range(2)]
    cc_inq = [nc.dram_tensor("cc_inq%d" % i, [128, 2048], BF16, kind="Internal").ap() for i in range(4)]
    cc_outq = [nc.dram_tensor("cc_outq%d" % i, [512, 2048], BF16, kind="Internal").ap() for i in range(4)]

    try:
      with ExitStack() as st:
        P = Prog(nc, st)
        state = {}

        def checkpoint(k, dumps):
            if stop != k:
                return
            bar = [n.id for n in P.nodes]
            fins = []
            for name, ap, shape, dt_ in dumps:
                d = nc.dram_tensor("dbg_" + name, shape, dt_, kind="ExternalOutput").ap()
                fins.append(P.dma("dbg_" + name, d, ap, extra=bar, nbytes=1 << 20))
            P.finish(fins)
            raise _Stop()

        sb = lambda name, shape, dt: st.enter_context(nc.sbuf_tensor(name, shape, dt))
        R1 = 32776
        R1N = 20480
        R3 = R1 + R1N
        RM = R3 + 16384
        ABF = sb("abf", [128, RM + 8192], BF16)
        AF32 = sb("af32", [128, 8192], F32)
        small = sb("small", [128, 256], F32)
        small2 = sb("small2", [128, 192], F32)
        identf = sb("identf", [128, 128], F32)
        ident = sb("ident_sb", [128, 128], BF16)
        f128 = sb("f128_sb", [128, 256], BF16)
        f64 = sb("f64_sb", [128, 128], BF16)
        dconv = sb("dconv", [128, 512], BF16)
        wgate = sb("wgate_sb", [128, 512], BF16)
        junk2 = sb("junk", [128, 2048], BF16)
        JB = [Buf("junk0"), Buf("junk1")]
        jcnt = [0]

        def next_junk():
            i = jcnt[0] % 2
            jcnt[0] += 1
            return junk2[:, i * 1024:(i + 1) * 1024], JB[i]
        upc = sb("upc", [128, 264], BF16)
        ones = sb("ones", [1, 128], F32)
        bc = sb("bc", [128, 2 * D], F32)
        vbfs = sb("vbfs", [128, 4 * 512], BF16)

        ufT = ABF[:, 0:8192]
        upad = ABF[:, 8192:16392]
        sgl = ABF[:, 16392:24584]
        sgf = ABF[:, 24584:32776]
        Wp = ABF[:, R1:R1 + 7168].rearrange("p (k n) -> p k n", k=8)
        Wpc = ABF[:, R1 + 7168:R1 + 8192].rearrange("p (k n) -> p k n", k=8)
        xn = [ABF[:, R1 + 8192 + i * 1024:R1 + 8192 + (i + 1) * 1024] for i in range(4)]
        xT = [ABF[:, R1 + 12288 + i * 4096:R1 + 12288 + (i + 1) * 4096].rearrange("p (k n) -> p k n", k=8)
              for i in range(2)]
        gtab = ABF[:, R1:R1 + 16384]
        woutp = ABF[:, R1:R1 + 8192].rearrange("p (k n) -> p k n", k=8)
        wfourp = ABF[:, R1 + 8192:R1 + 10240].rearrange("p (k n) -> p k n", k=4)
        yfg = ABF[:, R1 + 10240:R1 + 18432].rearrange("p (k n) -> p k n", k=4)
        hst = ABF[:, R3:R3 + 8192]
        Z0 = ABF[:, R3:R3 + 16384]
        Yb = ABF[:, RM:RM + 8192]
        rows = ABF[0:1, R3:R3 + 4096].bitcast(F32)
        ylgall = ABF[:, R3:R3 + 8192].rearrange("p (k n) -> p k n", k=4)
        mixall = ABF[:, R3 + 8192:R3 + 16384].rearrange("p (k n) -> p k n", k=4)
        mixT = ufT

        Q = [AF32[:, i * 2048:(i + 1) * 2048] for i in range(4)]
        QB = [Buf("q%d" % i) for i in range(4)]

        cv = small[:, 0:16]
        sc = small[:, 16:32]
        bada_sb = small[:, 32:64]
        modT = small[:, 64:96]
        gain_sb = small[:, 96:104]
        Gx = small[:, 104:112]
        Gc = small[:, 112:120]
        lv = small[:, 120:136]
        bias_sb = small[:, 136:150]
        mhalf = small[:, 154:155]
        mhalf2 = small[:, 154:156]
        sp_t = small[:, 160:176]
        negKh = small[:, 176:178]
        hbr = small[:, 178:180]
        hbi = small[:, 180:182]
        h0 = small[:, 182:184]
        zero1 = small[:, 184:185]
        negK = small[:, 186:188]
        carry = small[:, 190:198]
        ssq = [small[:, 200 + 2 * i:202 + 2 * i] for i in range(4)]
        rstd = [small[:, 208 + 2 * i:210 + 2 * i] for i in range(4)]
        ss5 = [small[:, 216 + i:217 + i] for i in range(2)]
        rs5 = [small[:, 218 + i:219 + i] for i in range(2)]

        cv4 = small2[:, 0:32]
        sc4 = small2[:, 32:64]
        badas = small2[:, 64:76]
        modS = small2[:, 76:92]
        modX = small2[:, 96:120]
        modC = small2[:, 120:144]
        sh2 = small2[:, 144:160]
        pb = [st.enter_context(nc.psum_tensor("pb%d" % i, [128, 512], F32)) for i in range(8)]
        PB = [Buf("pb%d" % i) for i in range(8)]

        names = ["ufT", "sgf", "Wp", "Wpc", "xn0", "xn1", "xn2", "xn3", "xT0", "xT1",
                 "small", "consts", "dconv", "wgate", "upc", "rows", "ones", "bc", "Z0", "Y",
                 "gtab", "woutp", "wfourp", "modT", "modS", "modX", "modC", "sh2", "s2in", "identf", "bias", "h0", "lruc", "mh",
                 "ssq0", "ssq1", "ssq2", "ssq3", "rstd0", "rstd1", "rstd2", "rstd3", "ss50", "ss51", "rs50", "rs51",
                 "vbf0", "vbf1", "vbf2", "vbf3", "xo0", "xo1", "res0", "res1"]
        B = {n: Buf(n) for n in names}
        UP = [Buf("upad%d" % i) for i in range(16)]
        SG = [Buf("sgl%d" % i) for i in range(8)]
        HS = [Buf("hst%d" % i) for i in range(16)]
        CAR = [Buf("car%d" % i) for i in range(8)]

        def mm_group(bank_buf, steps, first_extra=(), reads=(), dur=None):
            prev = None
            for si, fn in enumerate(steps):
                if si == 0:
                    prev = P.ins("tensor", fn, reads=reads, writes=[bank_buf], extra=first_extra, dur=dur)
                else:
                    prev = P.ins("tensor", fn, extra=[prev], dur=dur)
                    for b in reads:
                        b.readers.append(prev)
            bank_buf.writers = [prev]
            return prev

        def mm_chain_shared(bank_buf, steps, reads, dur):
            prev_w = list(bank_buf.writers)
            tmp = Buf("tmp")
            tmp.writers = prev_w
            tmp.readers = list(bank_buf.readers)
            last = mm_group(tmp, steps, reads=reads, dur=dur)
            bank_buf.writers = [last]
            return last

        cids = [P.dma("c0", ident[:], ident_d[:, :], nbytes=32768),
                P.dma("c1", f128[:], f128_d[:, :], nbytes=65536),
                P.dma("c2", f64[:], f64_d[:, :], nbytes=32768)]
        B["consts"].writers = list(cids)
        B["s2in"].writers = [P.dma("c3", cv4[:, 0:16], cvec2[:, :], nbytes=8192),
                             P.dma("c4", badas, bada_s[:, :], nbytes=8192)]
        sids = [P.dma("c5", gain_sb, gaincol[:, :], nbytes=4096),
                P.dma("c6", lv, lruvec[:, :], nbytes=8192)]
        B["small"].writers = list(sids)
        rids = [P.dma("c8", rows[:, D:2 * D], fgrow[:, :], nbytes=4096)]
        B["rows"].writers = list(rids)
        P.ins("gpsimd", lambda e: e.memset(mhalf2, -0.5), pwrites=[B["mh"]], dur=100)
        P.ins("gpsimd", lambda e: e.memset(zero1, 0.0), pwrites=[B["mh"]], dur=100)
        P.ins("gpsimd", lambda e: e.memset(ones[:], 1.0), writes=[B["ones"]], dur=100)
        P.ins("gpsimd", lambda e: e.memset(upad[:, 0:2], 0.0), pwrites=[UP[0]], dur=100)
        P.ins("gpsimd", lambda e: e.memset(upad[:, 8194:8200], 0.0), pwrites=[UP[15]], dur=100)
        P.ins("gpsimd", lambda e: e.memset(upc[:, 0:2], 0.0), pwrites=[B["upc"]], dur=100)
        P.ins("gpsimd", lambda e: e.memset(upc[:, 258:264], 0.0), pwrites=[B["upc"]], dur=100)
        wave = [0]

        YF0 = Yb.bitcast(F32)
        SQ = [YF0[:, 0:2048], YF0[:, 2048:4096]]
        SQB = [Buf("sq0"), Buf("sq1")]
        R3F = ABF[:, R3 + 4096:R3 + 16384].bitcast(F32)
        WAB = [Buf("wa0"), Buf("wa1")]

        def stage_load(src_ap, ncols, nk=8, extra=()):
            w = wave[0]
            wave[0] += 1
            q = SQ[w % 2]
            qB = SQB[w % 2]
            ws3 = q[:, 0:nk * ncols].rearrange("p (k n) -> p k n", k=nk)
            P.dma("w%d" % (w % 2), ws3, src_ap, writes=[qB], nbytes=128 * nk * ncols * 4, extra=extra)
            return ws3, qB

        P.ins("scalar", lambda e: e.activation(out=sc4[:, 0:16], in_=cv4[:, 0:16], func=AF.Silu), reads=[B["s2in"]],
              writes=[B["modT"]], dur=300, tset=T_SILU)
        sc2v = sc4[:, 0:16].rearrange("p (k v) -> p k v", v=2)
        for wv in range(2):
            wsl = R3F[:, wv * 3072:(wv + 1) * 3072].rearrange("p (k n) -> p k n", k=8)
            wB = [WAB[wv]]
            P.dma("wa%d" % wv, wsl, wada_s[:, wv * 384:(wv + 1) * 384].rearrange("(k p) n -> p k n", p=128), writes=wB,
                  nbytes=3 << 19)
            for j3 in range(3):
                jj = wv * 3 + j3
                mm_chain_shared(PB[7], [(lambda e, jj=jj, j3=j3, kc=kc, wsl=wsl: e.matmul(
                    pb[7][:, 2 * jj:2 * jj + 2], lhsT=wsl[:, kc, j3 * 128:(j3 + 1) * 128], rhs=sc2v[:, kc, :],
                    start=(kc == 0), stop=(kc == 7))) for kc in range(8)], reads=wB + [B["modT"]], dur=230)
        P.ins("vector", lambda e: e.memset(modS, 0.0), writes=[B["modS"]], dur=100)
        P.ins("vector", lambda e: e.tensor_tensor(out=modS[:, 0:12].rearrange("p (v j) -> p j v", v=2),
                                                  in0=pb[7][:, 0:12].rearrange("p (j v) -> p j v", v=2),
                                                  in1=badas.rearrange("p (j v) -> p j v", v=2), op=ALU.add),
              reads=[PB[7], B["s2in"]], writes=[B["modS"]], dur=200)
        d_ = P.dma("cmi", cm_in[:, :], modS, reads=[B["modS"]], nbytes=8192)
        ag0 = P.cc("ag", lambda e: e.collective_compute("AllGather", ALU.bypass, replica_groups=[[0, 1, 2, 3], [4, 5, 6, 7]],
                                                         ins=[cm_in], outs=[cm_out]), extra=[d_], dur=38000.0)
        cmv = cm_out[:, 0:12].rearrange("(r p) (v j) -> p r v j", p=128, v=2)
        P.dma("cmx", modX.rearrange("p (r v j) -> p r v j", r=4, v=1), cmv[:, :, 0:1, :], extra=[ag0], writes=[B["modX"]],
              nbytes=16384)
        P.dma("cmc", modC.rearrange("p (r v j) -> p r v j", r=4, v=1), cmv[:, :, 1:2, :], extra=[ag0], writes=[B["modC"]],
              nbytes=16384)
        P.ins("vector", lambda e: e.scalar_tensor_tensor(out=Gx, in0=modX[:, 8:16], scalar=1.0, in1=gain_sb,
                                                         op0=ALU.add, op1=ALU.mult),
              reads=[B["modX"], B["small"]], pwrites=[B["modT"]], dur=200)
        P.ins("vector", lambda e: e.scalar_tensor_tensor(out=Gc, in0=modC[:, 8:16], scalar=1.0, in1=gain_sb,
                                                         op0=ALU.add, op1=ALU.mult),
              reads=[B["modC"], B["small"]], pwrites=[B["modT"]], dur=200)
        sh2v = sh2.rearrange("p (k s) -> p k s", s=2)
        P.ins("vector", lambda e: e.tensor_copy(out=sh2v[:, :, 0], in_=modX[:, 0:8]), reads=[B["modX"]],
              pwrites=[B["sh2"]], dur=150)
        P.ins("vector", lambda e: e.tensor_copy(out=sh2v[:, :, 1], in_=modC[:, 0:8]), reads=[B["modC"]],
              pwrites=[B["sh2"]], dur=150)
        col_ranges = [(0, 256), (256, 512), (512, 768), (768, 896)]
        for wi, (c0, c1) in enumerate(col_ranges):
            wc = c1 - c0
            ws3, qB = stage_load(w_in_c[:, c0:c1].rearrange("(k p) n -> p k n", p=128), wc)
            for kc in range(8):
                P.ins("vector", lambda e, kc=kc, ws3=ws3, c0=c0, c1=c1: e.tensor_scalar(
                    out=Wp[:, kc, c0:c1], in0=ws3[:, kc, :], scalar1=Gx[:, kc:kc + 1], scalar2=None, op0=ALU.mult),
                    reads=[qB, B["modT"]], pwrites=[B["Wp"]], dur=dve_ns(wc))
                if c0 == 0:
                    P.ins("gpsimd", lambda e, kc=kc, ws3=ws3: e.tensor_scalar(
                        out=Wpc[:, kc, :], in0=ws3[:, kc, 128:256], scalar1=Gc[:, kc:kc + 1], scalar2=None,
                        op0=ALU.mult), reads=[qB, B["modT"]], pwrites=[B["Wpc"]], dur=pool_ns(128))
            for cc_ in range(wc // 128):
                ch = c0 // 128 + cc_
                mm_chain_shared(PB[4], [
                    (lambda e, kc=kc, ws3=ws3, cc_=cc_, ch=ch: e.matmul(
                        pb[4][:, 2 * ch:2 * ch + 2], lhsT=ws3[:, kc, cc_ * 128:(cc_ + 1) * 128],
                        rhs=sh2v[:, kc, :], start=(kc == 0), stop=(kc == 7))) for kc in range(8)],
                    reads=[qB, B["sh2"]], dur=230)
        P.ins("vector", lambda e: e.tensor_copy(out=bias_sb, in_=pb[4][:, 0:14]), reads=[PB[4]], writes=[B["bias"]],
              dur=200)

        P.ins("vector", lambda e: e.tensor_copy(out=identf[:], in_=ident[:]), reads=[B["consts"]], writes=[B["identf"]],
              dur=dve_ns(128))
        for h in range(2):
            mm_chain_shared(PB[5 + h], [(lambda e, h=h, jq=jq: e.matmul(
                pb[5 + h][0:1, jq * 128:(jq + 1) * 128], lhsT=modX[:, 16 + 4 * h + jq:17 + 4 * h + jq], rhs=identf[:],
                start=True, stop=True)) for jq in range(4)], reads=[B["modX"], B["identf"]], dur=pe_ns(128, 4))
            P.ins("vector", lambda e, h=h: e.tensor_copy(out=rows[:, h * 512:(h + 1) * 512], in_=pb[5 + h][0:1, :]),
                  reads=[PB[5 + h]], pwrites=[B["rows"]], dur=700)
        for q_ in range(4):
            bk = 5 + (q_ % 2)
            P.ins("tensor", lambda e, q_=q_, bk=bk: e.matmul(pb[bk][:, :], lhsT=ones[0:1, :],
                                                             rhs=rows[0:1, q_ * 512:(q_ + 1) * 512], start=True, stop=True),
                  reads=[B["ones"], B["rows"]], writes=[PB[bk]], dur=pe_ns(512, 4))
            P.ins("scalar", lambda e, q_=q_, bk=bk: e.copy(out=bc[:, q_ * 512:(q_ + 1) * 512], in_=pb[bk][:, :]),
                  reads=[PB[bk]], pwrites=[B["bc"]], dur=act_ns(512))

        for k in range(4):
            P.ins("vector", lambda e, k=k: e.tensor_scalar(out=dconv[:, k * 128:(k + 1) * 128], in0=ident[:],
                                                           scalar1=lv[:, k:k + 1], scalar2=None, op0=ALU.mult),
                  reads=[B["consts"], B["small"]], pwrites=[B["dconv"]], dur=dve_ns(128))
        ws3, qB = stage_load(w_gate[:, :].rearrange("p (k n) -> p k n", k=1), 512, nk=1)
        P.ins("vector", lambda e, ws3=ws3: e.tensor_copy(out=wgate[:], in_=ws3[:, 0, :]), reads=[qB],
              writes=[B["wgate"]], dur=dve_ns(512))
        lam = lv[:, 9:11]
        t_na, t_z, t_d, t_s, t_q, t_p, t_m = [sp_t[:, 2 * i:2 * i + 2] for i in range(7)]
        V = lambda fn: P.ins("vector", fn, reads=[B["small"]], writes=[B["lruc"]], dur=150)
        V(lambda e: e.tensor_scalar(out=t_m, in0=lam, scalar1=-1.0, scalar2=None, op0=ALU.mult))
        V(lambda e: e.tensor_tensor(out=t_na, in0=lam, in1=t_m, op=ALU.min))
        P.ins("scalar", lambda e: e.activation(out=t_z, in_=t_na, func=AF.Exp), reads=[B["lruc"]], writes=[B["lruc"]],
              dur=300, tset=T_EXP)
        V(lambda e: e.tensor_scalar(out=t_d, in0=t_z, scalar1=2.0, scalar2=None, op0=ALU.add))
        V(lambda e: e.reciprocal(out=t_d, in_=t_d))
        V(lambda e: e.tensor_tensor(out=t_s, in0=t_z, in1=t_d, op=ALU.mult))
        V(lambda e: e.tensor_tensor(out=t_q, in0=t_s, in1=t_s, op=ALU.mult))
        V(lambda e: e.tensor_scalar(out=t_p, in0=t_q, scalar1=1.0 / 11.0, scalar2=None, op0=ALU.mult))
        for cst in [1.0 / 9, 1.0 / 7, 1.0 / 5, 1.0 / 3]:
            V(lambda e, cst=cst: e.scalar_tensor_tensor(out=t_p, in0=t_p, scalar=cst, in1=t_q, op0=ALU.add, op1=ALU.mult))
        V(lambda e: e.scalar_tensor_tensor(out=t_p, in0=t_p, scalar=1.0, in1=t_s, op0=ALU.add, op1=ALU.mult))
        V(lambda e: e.tensor_scalar(out=t_m, in0=lam, scalar1=-1.0, scalar2=0.0, op0=ALU.mult, op1=ALU.max))
        V(lambda e: e.tensor_scalar(out=t_m, in0=t_m, scalar1=-4.0, scalar2=None, op0=ALU.mult))
        V(lambda e: e.scalar_tensor_tensor(out=negKh, in0=t_p, scalar=-8.0, in1=t_m, op0=ALU.mult, op1=ALU.add))
        V(lambda e: e.tensor_scalar(out=negK, in0=negKh, scalar1=2.0, scalar2=None, op0=ALU.mult))
        V(lambda e: e.tensor_scalar(out=hbr, in0=lv[:, 5:7], scalar1=0.5, scalar2=None, op0=ALU.mult))
        V(lambda e: e.tensor_scalar(out=hbi, in0=lv[:, 7:9], scalar1=0.5, scalar2=None, op0=ALU.mult))

        checkpoint(0, [("small", small[:], [128, 256], F32), ("wp", ABF[:, R1:R1 + 8192], [128, 8192], BF16),
                       ("bc", bc[:], [128, 2 * D], F32), ("dconv", dconv[:], [128, 512], BF16),
                       ("wgate", wgate[:], [128, 512], BF16)])

        pTbank = [0, 1, 5, 6]
        pT = [pb[i][:].bitcast(BF16) for i in pTbank]
        accbank = [2, 3, 4, 7]
        st1 = {"grp": 0, "tile": 0, "acc": 0, "pacc": 0}
        xb_t = xb.rearrange("(t p) f -> p t f", p=128)

        def dreg(e, eng, name):
            key = eng + name
            if key not in state:
                g_ = e.partition_id() % 4
                expr = {"g": g_, "pr": g_ // 2, "npr": 1 - g_ // 2, "lo": g_ % 2, "nlo": 1 - g_ % 2}[name]
                state[key] = e.snap(expr, min_val=0, max_val=3 if name == "g" else 1)
            return state[key]
        xb4 = xb.rearrange("(o t p) f -> p o t f", o=4, p=128)

        def proj_pass(src_fn, ntok, Wsrc, WB, specs, dyn_src=None):
            ngroups = ntok // 256
            n_mm = 512 if ntok >= 512 else ntok
            for g in range(ngroups):
                gi = st1["grp"]
                st1["grp"] += 1
                xs = Q[gi % 4].rearrange("p (t f) -> p t f", t=2)
                xsB = QB[gi % 4]
                if dyn_src is not None:
                    P.dma("x%d" % (gi % 4), None, None, writes=[xsB], nbytes=1 << 20,
                          dyn=lambda e, g=g, xs=xs: (xs.rearrange("p (o t) f -> p o t f", o=1), dyn_src(e, g)))
                else:
                    P.dma("x%d" % (gi % 4), xs, src_fn(g), writes=[xsB], nbytes=1 << 20)
                sq = ssq[gi % 4]
                sqB = B["ssq%d" % (gi % 4)]
                rs = rstd[gi % 4]
                rsB = B["rstd%d" % (gi % 4)]
                P.begin(sqB)
                for t in range(2):
                    jk, jkB = next_junk()
                    P.ins("scalar", lambda e, xs=xs, t=t, sq=sq, jk=jk: e.activation(out=jk, in_=xs[:, t, :], func=AF.Square,
                                                                                   accum_out=sq[:, t:t + 1]),
                          reads=[xsB], writes=[jkB], pwrites=[sqB], dur=act_ns(1024) + 100)
                P.ins("gpsimd", lambda e, sq=sq, rs=rs: e.tensor_scalar(out=rs, in0=sq, scalar1=1.0 / D, scalar2=EPS,
                                                                      op0=ALU.mult, op1=ALU.add),
                      reads=[sqB], writes=[rsB], dur=200)
                P.ins("gpsimd", lambda e, rs=rs: e.tensor_tensor(out=rs, in0=rs, in1=mhalf2, op=ALU.pow),
                      reads=[B["mh"]], writes=[rsB], dur=650)
                for t in range(2):
                    ti = st1["tile"]
                    st1["tile"] += 1
                    xnb = xn[ti % 4]
                    xnB = B["xn%d" % (ti % 4)]
                    P.ins("vector", lambda e, xs=xs, t=t, xnb=xnb, rs=rs: e.tensor_scalar(
                        out=xnb, in0=xs[:, t, :], scalar1=rs[:, t:t + 1], scalar2=None, op0=ALU.mult),
                        reads=[xsB, rsB], writes=[xnB], dur=dve_ns(1024, 0.56))
                    pTb = pT[ti % 4]
                    pTB = PB[pTbank[ti % 4]]
                    mm_group(pTB, [(lambda e, kc=kc, xnb=xnb, pTb=pTb: e.transpose(
                        out=pTb[:, kc * 128:(kc + 1) * 128], in_=xnb[:, kc * 128:(kc + 1) * 128], identity=ident[:]))
                        for kc in range(8)], reads=[xnB, B["consts"]], dur=pe_ns(128))
                    tok_in_pass = (g * 2 + t) * 128
                    ai = st1["acc"]
                    xTb = xT[ai % 2]
                    xTB = B["xT%d" % (ai % 2)]
                    off = tok_in_pass % n_mm
                    if off == 0:
                        P.begin(xTB)
                    pT3 = pTb.rearrange("p (k n) -> p k n", k=8)
                    if False:
                        pass
                    else:
                        P.ins("vector", lambda e, pT3=pT3, xTb=xTb, off=off: e.tensor_copy(
                            out=xTb[:, :, off:off + 128], in_=pT3), reads=[pTB], pwrites=[xTB], dur=dve_ns(1024, 0.9))
                    if off + 128 == n_mm:
                        t0 = tok_in_pass + 128 - n_mm
                        for col0, evac in specs:
                            bk = accbank[st1["pacc"] % 4]
                            st1["pacc"] += 1
                            mm_group(PB[bk], [(lambda e, kc=kc, xTb=xTb, col0=col0, bk=bk, n_mm=n_mm: e.matmul(
                                pb[bk][:, 0:n_mm], lhsT=Wsrc[:, kc, col0:col0 + 128], rhs=xTb[:, kc, 0:n_mm],
                                start=(kc == 0), stop=(kc == 7))) for kc in range(8)],
                                reads=[WB, xTB], dur=pe_ns(n_mm))
                            evac(pb[bk][:, 0:n_mm], PB[bk], t0, n_mm)
                        st1["acc"] += 1

        def evac_act(dst_fn, dstB_fn, func, bias_col, tset):
            def f(ps, psB, t0, n):
                P.ins("scalar", lambda e: e.activation(out=dst_fn(t0, n), in_=ps, func=func, bias=bias_col),
                      reads=[psB, B["bias"]], pwrites=[dstB_fn(t0)], dur=act_ns(n), tset=tset)
            return f

        proj_pass(lambda g: ctxb[g * 256:(g + 1) * 256, :].rearrange("(t p) f -> p t f", p=128), CTX, Wpc, B["Wpc"],
                  [(0, evac_act(lambda t0, n: upc[:, 2 + t0:2 + t0 + n], lambda t0: B["upc"], AF.Identity,
                                bias_sb[:, 3:4], T_ANY))])
        proj_pass(lambda g: xb[g * 256:(g + 1) * 256, :].rearrange("(t p) f -> p t f", p=128), SEQ, Wp, B["Wp"],
                  [(0, evac_act(lambda t0, n: ufT[:, t0:t0 + n], lambda t0: B["ufT"], AF.Identity, bias_sb[:, 0:1], T_ANY)),
                   (128, evac_act(lambda t0, n: upad[:, 2 + t0:2 + t0 + n], lambda t0: UP[t0 // 512], AF.Identity,
                                  bias_sb[:, 2:3], T_ANY)),
                   (256, evac_act(lambda t0, n: sgl[:, t0:t0 + n], lambda t0: SG[t0 // 1024], AF.Silu, bias_sb[:, 4:5],
                                  T_SILU))])

        def own_src(e, g):
            return xb4[:, bass.ds(dreg(e, "sync", "g"), 1), 2 * g:2 * g + 2, :]
        proj_pass(None, OWN, Wp, B["Wp"],
                  [(384 + 128 * fc, evac_act(lambda t0, n, fc=fc: sgf[:, fc * 2048 + t0:fc * 2048 + t0 + n],
                                             lambda t0: B["sgf"], AF.Silu, bias_sb[:, 6 + 2 * fc:7 + 2 * fc], T_SILU))
                   for fc in range(4)], dyn_src=own_src)

        checkpoint(1, [("uft", ufT, [128, 8192], BF16), ("upad", upad, [128, 8200], BF16),
                       ("sgl", sgl, [128, 8192], BF16), ("sgf", sgf, [128, 8192], BF16),
                       ("upc", upc[:], [128, 264], BF16)])

        Z3 = Z0.rearrange("p (j t) -> p j t", t=128)
        Yfull = ABF[:, R3 + 8192:R3 + 24576]
        Y3 = Yfull.rearrange("p (j c) -> p j c", c=128)
        mixv = mixT.rearrange("p (k2 k1) -> p k1 k2", k1=64)
        fb = [2, 3, 4, 7, 0, 1, 5, 6]
        ev = [0]

        def nbank():
            b_ = fb[ev[0] % 8]
            ev[0] += 1
            return b_

        def evac_copy(out_ap, in_ap, psB, n, **kw):
            if ev[0] % 2 == 0:
                return P.ins("scalar", lambda e: e.copy(out=out_ap, in_=in_ap), reads=[psB], dur=act_ns(n), **kw)
            return P.ins("vector", lambda e: e.tensor_copy(out=out_ap, in_=in_ap), reads=[psB], dur=dve_ns(n, 0.9), **kw)

        r1_done = []
        for n_ in ["Wp", "Wpc", "xn0", "xn1", "xn2", "xn3", "xT0", "xT1"]:
            r1_done += B[n_].writers + B[n_].readers + B[n_].war
        for q_ in range(4):
            P.dma("g%d" % q_, gtab[:, q_ * 4096:(q_ + 1) * 4096], gtab_d[:, q_ * 4096:(q_ + 1) * 4096],
                  pwrites=[B["gtab"]], extra=r1_done, nbytes=1 << 20)
        RG = [[0, 1, 2, 3], [4, 5, 6, 7]]
        ag_m, ag_l = [], []
        r3_pre = B["rows"].writers + B["rows"].readers
        for b_ in WAB:
            r3_pre += b_.writers + b_.readers
        p0_stage = []
        for b_ in SQB:
            p0_stage += b_.writers + b_.readers
        for tg in range(32):
            bk = nbank()
            steps = []
            for tt in range(4):
                t2 = tg * 4 + tt
                for ri in range(2):
                    steps.append(lambda e, t2=t2, tt=tt, ri=ri, bk=bk: e.matmul(
                        pb[bk][64 * ri:64 * ri + 64, tt * 128:(tt + 1) * 128], lhsT=ufT[:, t2::128],
                        rhs=f128[:, ri * 128:(ri + 1) * 128], start=True, stop=True))
            mm_group(PB[bk], steps, reads=[B["ufT"], B["consts"]], dur=pe_ns(128))
            evac_copy(Z3[:, :, tg * 4:(tg + 1) * 4], pb[bk][:, :].rearrange("p (t j) -> p j t", t=4), PB[bk], 512,
                      pwrites=[B["Z0"]], extra=r3_pre)
        mix_parts = [[], []]
        P.begin(B["Y"])
        for half in (1, 0):
            y_extra = p0_stage if half == 1 else (list(r3_pre) + B["Z0"].writers + B["Z0"].readers)
            for jg in range(16):
                bk = nbank()
                mm_group(PB[bk], [(lambda e, j=half * 64 + jg * 4 + jj, jj=jj, bk=bk: e.matmul(
                    pb[bk][:, jj * 128:(jj + 1) * 128], lhsT=Z3[:, j, :], rhs=f64[:, :], start=True, stop=True))
                    for jj in range(4)], reads=[B["Z0"], B["consts"]], dur=pe_ns(128))
                evac_copy(Yfull[:, half * 8192 + jg * 512:half * 8192 + (jg + 1) * 512], pb[bk][:, :], PB[bk], 512,
                          pwrites=[B["Y"]], extra=y_extra)
        for kq in range(16):
            bk = nbank()
            steps = []
            for kk in range(4):
                k1 = kq * 4 + kk
                for ri in range(2):
                    steps.append(lambda e, k1=k1, kk=kk, ri=ri, bk=bk: e.matmul(
                        pb[bk][:, kk * 128:(kk + 1) * 128], lhsT=Y3[:, :, ri * 64 + k1],
                        rhs=gtab[:, ri * SEQ + k1 * 128:ri * SEQ + (k1 + 1) * 128], start=(ri == 0), stop=(ri == 1)))
            mm_group(PB[bk], steps, reads=[B["Y"], B["gtab"]], dur=pe_ns(128))
            srcv = pb[bk][:, :].rearrange("p (k1 k2) -> p k1 k2", k1=4)
            uf_war = B["ufT"].writers + B["ufT"].readers
            mix_parts[0].append(evac_copy(mixv[:, kq * 4:(kq + 1) * 4, :], srcv, PB[bk], 512, extra=uf_war))
        checkpoint(3, [("mixed", mixT, [128, 8192], BF16)])
        mix4 = mixT.rearrange("p (h q c) -> p h q c", h=4, q=4)
        ccm_ids = []
        for q_ in range(4):
            d_ = ccm_ids.append(None) or P.dma("ccm%d" % q_, cc_inq[q_].rearrange("p (h c) -> p h c", h=4), mix4[:, :, q_, :],
                       extra=mix_parts[0] + mix_parts[1], nbytes=1 << 19)
            ccm_ids[-1] = d_
            ag_m.append(P.cc("ag", lambda e, q_=q_: e.collective_compute(
                "AllGather", ALU.bypass, replica_groups=RG, ins=[cc_inq[q_]], outs=[cc_outq[q_]]), extra=[d_], dur=27000.0))

        r3_done = list(r3_pre) + B["Z0"].writers + B["Z0"].readers + B["Z0"].war
        lru_first = len(P.nodes)
        LBS = [[Buf("lt%d_%d" % (s_, i)) for i in range(4)] for s_ in range(4)]
        bsets = [[2, 3, 4], [7, 0, 1], [5, 6, 2], [3, 4, 7], [0, 1, 5], [6, 2, 3], [4, 7, 0], [1, 5, 6]]
        lst = {"bank": 0, "cslot": 0}

        def lru_step(up, upB_fn, T, SC, d, ci, prev, set_i, outs):
            s0 = ci * SC
            sub = min(SC, 512)
            nsub = SC // sub
            base = set_i * 2048
            v32 = AF32[:, base:base + SC]
            tr = AF32[:, base + 512:base + 512 + SC]
            tiw = AF32[:, base + 1024:base + 1024 + SC]
            a2 = AF32[:, base + 1536:base + 1536 + SC]
            LB = LBS[set_i]
            for b_ in LB:
                P.begin(b_)
            qwar = []
            for q_ in (QB[set_i],):
                qwar += q_.writers + q_.readers + q_.war
            for sbi in range(nsub):
                t0 = s0 + sbi * sub
                o = sbi * sub
                vslot = set_i
                vb = vbfs[:, vslot * 512:vslot * 512 + sub]
                vbB = B["vbf%d" % vslot]
                bkc, bkr, bki = bsets[lst["bank"] % 8]
                lst["bank"] += 1
                ups = list(dict.fromkeys([upB_fn(max(t0 - 2, 0)), upB_fn(t0), upB_fn(t0 + sub - 1),
                                          upB_fn(min(t0 + sub + 1, T - 1))]))
                mm_group(PB[bkc], [(lambda e, k=k, t0=t0, bkc=bkc, sub=sub: e.matmul(
                    pb[bkc][:, 0:sub], lhsT=dconv[:, k * 128:(k + 1) * 128], rhs=up[:, t0 + k:t0 + k + sub],
                    start=(k == 0), stop=(k == 3))) for k in range(4)],
                    reads=[B["dconv"]] + ups, dur=pe_ns(sub))
                P.ins("vector", lambda e, bkc=bkc, sub=sub, vb=vb: e.tensor_scalar(
                    out=vb, in0=pb[bkc][:, 0:sub], scalar1=lv[:, 4:5], scalar2=None, op0=ALU.add),
                    reads=[PB[bkc], B["small"]], writes=[vbB], dur=dve_ns(sub, 1.0))
                P.ins("tensor", lambda e, vb=vb, bkr=bkr, sub=sub: e.matmul(
                    pb[bkr][:, 0:sub], lhsT=wgate[:, (2 * d) * 128:(2 * d + 1) * 128], rhs=vb, start=True, stop=True),
                    reads=[B["wgate"], vbB], writes=[PB[bkr]], dur=pe_ns(sub))
                P.ins("tensor", lambda e, vb=vb, bki=bki, sub=sub: e.matmul(
                    pb[bki][:, 0:sub], lhsT=wgate[:, (2 * d + 1) * 128:(2 * d + 2) * 128], rhs=vb, start=True, stop=True),
                    reads=[B["wgate"], vbB], writes=[PB[bki]], dur=pe_ns(sub))
                tk_tr = P.ins("scalar", lambda e, o=o, bkr=bkr, sub=sub, tr=tr: e.activation(
                    out=tr[:, o:o + sub], in_=pb[bkr][:, 0:sub], func=AF.Tanh, bias=hbr[:, d:d + 1], scale=0.5),
                    reads=[PB[bkr], B["lruc"]], extra=LB[1].war + qwar, dur=act_ns(sub), tset=T_EXP)
                P.ins("scalar", lambda e, o=o, bki=bki, sub=sub, tiw=tiw: e.activation(
                    out=tiw[:, o:o + sub], in_=pb[bki][:, 0:sub], func=AF.Tanh, bias=hbi[:, d:d + 1], scale=0.5),
                    reads=[PB[bki], B["lruc"]], pwrites=[LB[2]], extra=qwar, dur=act_ns(sub), tset=T_EXP)
                tk_a2 = P.ins("scalar", lambda e, o=o, sub=sub, tr=tr, a2=a2: e.activation(
                    out=a2[:, o:o + sub], in_=tr[:, o:o + sub], func=AF.Exp, bias=negK[:, d:d + 1], scale=negK[:, d:d + 1]),
                    reads=[B["lruc"]], pwrites=[LB[3]], extra=[tk_tr] + qwar, dur=act_ns(sub), tset=T_EXP)
                P.ins("scalar", lambda e, o=o, sub=sub, tr=tr: e.activation(
                    out=tr[:, o:o + sub], in_=tr[:, o:o + sub], func=AF.Exp, bias=negKh[:, d:d + 1], scale=negKh[:, d:d + 1]),
                    reads=[B["lruc"]], pwrites=[LB[1]], extra=[tk_tr, tk_a2], dur=act_ns(sub), tset=T_EXP)
                tk_w = P.ins("vector", lambda e, o=o, sub=sub, tiw=tiw, vb=vb: e.scalar_tensor_tensor(
                    out=tiw[:, o:o + sub], in0=tiw[:, o:o + sub], scalar=1.0, in1=vb,
                    op0=ALU.add, op1=ALU.mult), reads=[LB[2], vbB], dur=dve_ns(sub, 1.0))
                LB[2].writers.append(tk_w)

            def post():
                P.ins("scalar", lambda e, a2=a2: e.activation(out=a2, in_=a2, func=AF.Sqrt, bias=0.25, scale=-0.25),
                      reads=[LB[3]], writes=[LB[3]], dur=act_ns(SC), tset=T_SQRT)
                P.ins("vector", lambda e, tiw=tiw, a2=a2: e.tensor_tensor(out=tiw, in0=tiw, in1=a2, op=ALU.mult),
                      reads=[LB[3]], writes=[LB[2]], dur=dve_ns(SC, 1.7))
                prev_ap, prevB = prev()
                if d == 1:
                    P.ins("vector", lambda e, v32=v32, tr=tr, tiw=tiw, prev_ap=prev_ap: e.tensor_tensor_scan(
                        out=v32[:, ::-1], data0=tr[:, ::-1], data1=tiw[:, ::-1], initial=prev_ap, op0=ALU.mult, op1=ALU.add),
                        reads=[prevB, LB[1], LB[2]], writes=[LB[0]], extra=qwar, dur=dve_ns(SC, 1.7))
                else:
                    P.ins("vector", lambda e, v32=v32, tr=tr, tiw=tiw, prev_ap=prev_ap: e.tensor_tensor_scan(
                        out=v32, data0=tr, data1=tiw, initial=prev_ap, op0=ALU.mult, op1=ALU.add),
                        reads=[prevB, LB[1], LB[2]], writes=[LB[0]], extra=qwar, dur=dve_ns(SC, 1.7))
                cidx = lst["cslot"] % 8
                lst["cslot"] += 1
                csrc = v32[:, 0:1] if d == 1 else v32[:, SC - 1:SC]
                P.ins("vector", lambda e, cidx=cidx, csrc=csrc: e.tensor_copy(out=carry[:, cidx:cidx + 1], in_=csrc),
                      reads=[LB[0]], writes=[CAR[cidx]], dur=150)
                outs(v32, LB[0], s0, SC)
                return (carry[:, cidx:cidx + 1], CAR[cidx])
            return post

        def lru_all(up, upB_fn, T, SC, init_f, init_b):
            nsc = T // SC
            cur = {"f": init_f, "b": init_b}
            for i in range(nsc):
                first = (2 * i < nsc)

                def outs(v32, vB, s0, n, first=first):
                    if T == CTX:
                        return
                    c_ = s0 // SC
                    if first:
                        P.ins("scalar", lambda e: e.copy(out=hst[:, s0:s0 + n], in_=v32), reads=[vB],
                              writes=[HS[c_]], extra=r3_done, dur=act_ns(n))
                    else:
                        P.ins("gpsimd", lambda e: e.tensor_tensor(out=v32, in0=v32, in1=hst[:, s0:s0 + n], op=ALU.add),
                              reads=[HS[c_]], writes=[vB], dur=pool_ns(n, 2.0))
                        P.ins("vector", lambda e: e.tensor_tensor(out=sgl[:, s0:s0 + n], in0=v32, in1=sgl[:, s0:s0 + n],
                                                                  op=ALU.mult), reads=[vB], writes=[SG[s0 // 1024]],
                              dur=dve_ns(n, 1.7))
                post_f = lru_step(up, upB_fn, T, SC, 0, i, lambda: cur["f"], i % 2, outs)
                post_b = lru_step(up, upB_fn, T, SC, 1, nsc - 1 - i, lambda: cur["b"], 2 + i % 2, outs)
                cur["f"] = post_f()
                cur["b"] = post_b()
            return cur["f"], cur["b"]

        cf_, cb_ = lru_all(upc, lambda t: B["upc"], CTX, CTX, (zero1, B["mh"]), (zero1, B["mh"]))
        P.ins("vector", lambda e: e.tensor_copy(out=h0[:, 0:1], in_=cf_[0]), reads=[cf_[1]], pwrites=[B["h0"]], dur=150)
        P.ins("vector", lambda e: e.tensor_copy(out=h0[:, 1:2], in_=cb_[0]), reads=[cb_[1]], pwrites=[B["h0"]], dur=150)
        lru_all(upad, lambda t: UP[t // 512], SEQ, 512, (h0[:, 0:1], B["h0"]), (h0[:, 1:2], B["h0"]))
        lru_nodes = list(range(lru_first, len(P.nodes)))
        checkpoint(2, [("ylg", sgl, [128, 8192], BF16), ("small", small[:], [128, 256], F32)])
        sg_all = []
        for b_ in SG:
            sg_all += b_.writers
        PAIRS = [[0, 1], [2, 3], [4, 5], [6, 7]]
        d_ = P.dma("cpa", None, None, extra=sg_all, nbytes=1 << 19, eng="scalar",
                   dyn=lambda e: (cpa_in.rearrange("p (a l c) -> p a l c", a=1, l=1),
                                  sgl.rearrange("p (a l c) -> p a l c", a=2, l=2)[
                                      :, bass.ds(dreg(e, "scalar", "pr"), 1), bass.ds(dreg(e, "scalar", "nlo"), 1), :]))
        ag_pair = P.cc("ag", lambda e: e.collective_compute(
            "AllGather", ALU.bypass, replica_groups=PAIRS, ins=[cpa_in], outs=[cpa_out]), extra=[d_], dur=9000.0)
        ag_oth = []
        sgl5 = sgl.rearrange("p (a h k c) -> p a h k c", a=2, h=2, k=2)
        for k in range(2):
            d_ = P.dma("c4i%d" % k, None, None, extra=sg_all, nbytes=1 << 19,
                       dyn=lambda e, k=k: (c4_in[k].rearrange("(h p) (a c) -> p a h c", p=128, a=1),
                                           sgl5[:, bass.ds(dreg(e, "sync", "npr"), 1), :, k, :]))
            ag_oth.append(P.cc("ag", lambda e, k=k: e.collective_compute(
                "AllGather", ALU.bypass, replica_groups=RG, ins=[c4_in[k]], outs=[c4_out[k]]), extra=[d_], dur=27000.0))

        g_done = B["gtab"].writers + B["gtab"].readers
        y_done = B["Y"].writers + B["Y"].readers + B["Y"].war
        YF = Yb.bitcast(F32)
        YQ = [YF[:, 0:2048], YF[:, 2048:4096]]
        YQB = [Buf("yq0"), Buf("yq1")]
        wave5 = [0]

        def stage_load(src_ap, ncols, nk=8, extra=()):
            w = wave5[0]
            wave5[0] += 1
            ws3_ = YQ[w % 2][:, 0:nk * ncols].rearrange("p (k n) -> p k n", k=nk)
            P.dma("wy%d" % (w % 2), ws3_, src_ap, writes=[YQB[w % 2]], nbytes=128 * nk * ncols * 4, extra=y_done)
            return ws3_, YQB[w % 2]
        ws3, qB = stage_load(w_four.rearrange("(k p) n -> p k n", p=128), 512, nk=4, extra=lru_nodes)
        P.ins("gpsimd", lambda e, ws3=ws3: e.tensor_copy(out=wfourp, in_=ws3), reads=[qB], writes=[B["wfourp"]],
              extra=g_done, dur=pool_ns(2048, 1.2))
        for wv in range(4):
            ws3, qB = stage_load(w_out[:, wv * 256:(wv + 1) * 256].rearrange("(k p) n -> p k n", p=128), 256,
                                 extra=lru_nodes)
            for fc in range(8):
                P.ins("vector", lambda e, fc=fc, wv=wv, ws3=ws3: e.tensor_tensor(
                    out=woutp[:, fc, wv * 256:(wv + 1) * 256], in0=ws3[:, fc, :], in1=bc[:, wv * 256:(wv + 1) * 256],
                    op=ALU.mult), reads=[qB, B["bc"]], pwrites=[B["woutp"]], extra=g_done, dur=dve_ns(256, 1.7))
        hs_done = list(r3_done)
        for b_ in HS:
            hs_done += b_.writers + b_.readers

        def get_pidown(e):
            if "pido" not in state:
                state["pido"] = e.snap((e.partition_id() % 4) * OWN, min_val=0, max_val=3 * OWN)
            return state["pido"]
        def get_pid512(e):
            if "pid512" not in state:
                state["pid512"] = e.snap((e.partition_id() % 4) * 512, min_val=0, max_val=3 * 512)
            return state["pid512"]
        ccq = [c_.rearrange("(r p) c -> p r c", p=128) for c_ in cc_outq]
        ga_m, ga_l = [], []
        gl_own = P.dma("gal0", None, None, extra=hs_done + sg_all, nbytes=1 << 19, eng="scalar",
                       dyn=lambda e: (ylgall[:, 0:1, :], sgl.rearrange("p (o c) -> p o c", o=4)[
                           :, bass.ds(dreg(e, "scalar", "g"), 1), :]))
        gl_par1 = P.dma("gal1", None, None, extra=hs_done + [ag_pair], nbytes=1 << 19, eng="scalar",
                        dyn=lambda e: (ylgall[:, 1:2, :], cpa_out.rearrange("(r p) c -> p r c", p=128)[
                            :, bass.ds(dreg(e, "scalar", "nlo"), 1), :]))
        gl_par = gl_par1
        gl_oth = []
        for k in range(2):
            gl_oth.append(P.dma("gal%d" % (2 + k), None, None, extra=hs_done + [ag_oth[k]], nbytes=1 << 19, eng="scalar",
                                dyn=lambda e, k=k: (ylgall[:, 2:4, k * 1024:(k + 1) * 1024].rearrange("p (a r) (h c) -> p a r h c", a=1, h=1),
                                                    c4_out[k].rearrange("(a r h p) c -> p a r h c", a=2, r=2, h=2)[
                                                        :, bass.ds(dreg(e, "scalar", "npr"), 1), :, bass.ds(dreg(e, "scalar", "lo"), 1), :])))
        for q_ in range(4):
            ga_m.append(P.dma("gam%d" % q_, None, None, extra=r3_done + y_done + [ag_m[q_]], nbytes=1 << 19,
                              dyn=lambda e, q_=q_: (mixall[:, :, q_ * 512:(q_ + 1) * 512].rearrange("p k (o c) -> p k o c", o=1),
                                                    cc_outq[q_].rearrange("(r p) (o c) -> p r o c", p=128, o=4)[
                                                        :, :, bass.ds(dreg(e, "sync", "g"), 1), :])))
        checkpoint(4, [("mixall", ABF[:, R3 + 8192:R3 + 16384], [128, 8192], BF16),
                       ("ylgall", ABF[:, R3:R3 + 8192], [128, 8192], BF16)])
        yfg_parts = [[] for _ in range(4)]

        def yf_quarter(tcn):
            for fc in range(4):
                bk = nbank()
                mm_group(PB[bk], [(lambda e, g=g, fc=fc, tcn=tcn, bk=bk: e.matmul(
                    pb[bk][:, :], lhsT=wfourp[:, g, fc * 128:(fc + 1) * 128], rhs=mixall[:, g, tcn * 512:(tcn + 1) * 512],
                    start=(g == 0), stop=(g == 3))) for g in range(4)], reads=[B["wfourp"]], first_extra=[ga_m[tcn]],
                    dur=pe_ns(512))
                yfg_parts[tcn].append(P.ins("vector", lambda e, fc=fc, tcn=tcn, bk=bk: e.tensor_tensor(
                    out=yfg[:, fc, tcn * 512:(tcn + 1) * 512], in0=pb[bk][:, :],
                    in1=sgf[:, fc * 2048 + tcn * 512:fc * 2048 + (tcn + 1) * 512], op=ALU.mult),
                    reads=[PB[bk], B["sgf"]], extra=g_done, dur=dve_ns(512, 1.8)))
        outs_ = []
        ufF = ufT.bitcast(F32)
        rmF = Yb.bitcast(F32)
        upF = ABF[:, 8192:16384].bitcast(F32)
        ccm_done = list(ccm_ids)
        stg_done = []
        for b_ in YQB:
            stg_done += b_.writers + b_.readers
        xo_regions = [(ufF, ccm_done + mix_parts[0] + mix_parts[1]), (rmF, stg_done + y_done),
                      (upF, lru_nodes), (AF32[:, 0:4096], lru_nodes)]
        xo_bufs = []
        XOB = []
        for gi_, (reg_, ex_) in enumerate(xo_regions):
            gB = Buf("xo_g%d" % gi_)
            P.dma("xo%d" % gi_, None, None, writes=[gB], extra=ex_, nbytes=1 << 21,
                  dyn=lambda e, gi_=gi_, reg_=reg_: (reg_.rearrange("p (o t f) -> p o t f", o=1, t=4),
                                                     xb4[:, bass.ds(dreg(e, "sync", "g"), 1), 4 * gi_:4 * gi_ + 4, :]))
            for i in range(4):
                xo_bufs.append((reg_[:, i * 1024:(i + 1) * 1024], ex_))
                XOB.append(gB)
        XP = [Buf("xp%d" % i) for i in range(16)]

        def acc_phase(tt, fcs, extra):
            xob, xo_extra = xo_bufs[tt]
            for half in range(2):
                bk = nbank()
                mm_group(PB[bk], [(lambda e, fc=fc, half=half, bk=bk, tt=tt: e.matmul(
                    pb[bk][:, :], lhsT=(yfg[:, fc, tt * 128:(tt + 1) * 128] if fc < 4 else ylgall[:, fc - 4, tt * 128:(tt + 1) * 128]),
                    rhs=woutp[:, fc, half * 512:(half + 1) * 512], start=(fc == fcs[0]), stop=(fc == fcs[-1]))) for fc in fcs],
                    reads=[B["woutp"]], first_extra=extra, dur=pe_ns(512))
                tk = P.ins("vector", lambda e, half=half, bk=bk, xob=xob: e.tensor_tensor(
                    out=xob[:, half * 512:(half + 1) * 512], in0=pb[bk][:, :], in1=xob[:, half * 512:(half + 1) * 512],
                    op=ALU.add), reads=[PB[bk], XOB[tt], XP[tt]], dur=dve_ns(512, 1.8))
                XP[tt].writers.append(tk)
        for tt in range(16):
            if tt % 4 == 0:
                yf_quarter(tt // 4)
            acc_phase(tt, [0, 1, 2, 3], yfg_parts[tt // 4])
        for tt in range(16):
            acc_phase(tt, [4, 5], [gl_own, gl_par1])
        XS = [Buf("xs5_%d" % i) for i in range(16)]
        XR = [Buf("xr5_%d" % i) for i in range(16)]
        for tt in range(16):
            xob, xo_extra = xo_bufs[tt]
            xpB = XP[tt]
            for half in range(2):
                bk = nbank()
                mm_group(PB[bk], [(lambda e, fc=fc, half=half, bk=bk, tt=tt: e.matmul(
                    pb[bk][:, :], lhsT=ylgall[:, fc - 4, tt * 128:(tt + 1) * 128],
                    rhs=woutp[:, fc, half * 512:(half + 1) * 512], start=(fc == 6), stop=(fc == 7))) for fc in (6, 7)],
                    reads=[B["woutp"]], first_extra=[gl_oth[tt // 8]], dur=pe_ns(512))
                tk = P.ins("vector", lambda e, half=half, bk=bk, xob=xob: e.tensor_tensor(
                    out=xob[:, half * 512:(half + 1) * 512], in0=pb[bk][:, :], in1=xob[:, half * 512:(half + 1) * 512],
                    op=ALU.add), reads=[PB[bk], xpB], dur=dve_ns(512, 1.8))
                xpB.writers.append(tk)
            s5 = small2[:, 160 + tt:161 + tt]
            r5 = small2[:, 176 + tt:177 + tt]
            jk, jkB = next_junk()
            P.ins("scalar", lambda e, xob=xob, s5=s5, jk=jk: e.activation(out=jk, in_=xob, func=AF.Square, accum_out=s5),
                  reads=[xpB], writes=[XS[tt], jkB], dur=act_ns(1024) + 100)
            P.ins("gpsimd", lambda e, s5=s5, r5=r5: e.tensor_scalar(out=r5, in0=s5, scalar1=1.0 / D, scalar2=EPS,
                                                                  op0=ALU.mult, op1=ALU.add), reads=[XS[tt]], writes=[XR[tt]], dur=200)
            P.ins("gpsimd", lambda e, r5=r5: e.tensor_tensor(out=r5, in0=r5, in1=mhalf, op=ALU.pow),
                  reads=[B["mh"]], writes=[XR[tt]], dur=650)
            P.ins("vector", lambda e, xob=xob, r5=r5: e.scalar_tensor_tensor(out=xob, in0=xob, scalar=r5,
                                                                            in1=bc[:, D:2 * D], op0=ALU.mult, op1=ALU.mult),
                  reads=[XR[tt], B["bc"]], writes=[xpB], dur=dve_ns(1024, 1.7))
            outs_.append(P.dma("out%d" % (tt % 4), y[tt * 128:(tt + 1) * 128, :], xob, reads=[xpB], nbytes=1 << 19))
        P.finish(outs_[-4:])
        state["sim_end"] = P.sim_end
        build_program.sim_end = P.sim_end
    except _Stop:
        pass
    return nc


_CACHE = {}


def _consts():
    if "c" in _CACHE:
        return _CACHE["c"]
    bf = ml_dtypes.bfloat16
    ident = np.eye(128, dtype=np.float32).astype(bf)
    c = np.arange(128)[:, None].astype(np.float64)
    j = np.arange(128)[None, :].astype(np.float64)
    C128 = np.cos(2 * np.pi * c * j / 128)
    S128 = np.sin(2 * np.pi * c * j / 128)
    f128 = np.concatenate([C128, -S128], axis=1).astype(np.float32).astype(bf)
    t1 = np.arange(64)[:, None].astype(np.float64)
    k1 = np.arange(64)[None, :].astype(np.float64)
    C64 = np.cos(2 * np.pi * t1 * k1 / 64)
    S64 = np.sin(2 * np.pi * t1 * k1 / 64)
    f64 = np.concatenate([np.concatenate([C64, -S64], axis=1), np.concatenate([S64, C64], axis=1)],
                         axis=0).astype(np.float32).astype(bf)
    t2 = np.arange(128)[:, None, None].astype(np.float64)
    k1 = np.arange(64)[None, :, None].astype(np.float64)
    k2 = np.arange(128)[None, None, :].astype(np.float64)
    ang = 2 * np.pi * ((t2 * (k1 + 64 * k2)) % SEQ) / SEQ
    Gc = (np.cos(ang) / 1024.0).reshape(128, SEQ)
    Gs = (np.sin(ang) / 1024.0).reshape(128, SEQ)
    gtab = np.concatenate([Gc, Gs], axis=1).astype(np.float32).astype(bf)
    _CACHE["c"] = (ident, f128, f64, gtab)
    return _CACHE["c"]


def _prep(x, c, ctx, c_ctx, w_ada, b_ada, norm_gain, w_in, w_four, conv_w, conv_b,
          w_rg, b_rg, w_ig, b_ig, lam, w_out, final_gain):
    f = lambda a: np.ascontiguousarray(np.asarray(a, dtype=np.float32))
    x, c, ctx, c_ctx = f(x), f(c), f(ctx), f(c_ctx)
    w_ada, b_ada, norm_gain, w_in = f(w_ada)[0], f(b_ada)[0], f(norm_gain)[0], f(w_in)[0]
    w_four, conv_w, conv_b = f(w_four)[0], f(conv_w)[0], f(conv_b)[0]
    w_rg, b_rg, w_ig, b_ig, lam = f(w_rg)[0], f(b_rg)[0], f(w_ig)[0], f(b_ig)[0], f(lam)[0]
    w_out, final_gain = f(w_out)[0], f(final_gain)
    ident, f128, f64, gtab = _consts()
    col = lambda v: np.ascontiguousarray(v.reshape(-1, 128).T)
    in_maps = []
    for core in range(8):
        b, g = core // 4, core % 4
        cvec2 = np.zeros((128, 8, 2), np.float32)
        cvec2[:, :, 0] = col(c[b])
        cvec2[:, :, 1] = col(c_ctx)
        cvec2 = np.ascontiguousarray(cvec2.reshape(128, 16))
        wada_s = np.ascontiguousarray(w_ada[:, 768 * g:768 * (g + 1)])
        bada_s = np.ascontiguousarray(np.repeat(col(b_ada[768 * g:768 * (g + 1)]), 2, axis=1))
        sl = slice(g * 128, (g + 1) * 128)
        w_in_c = np.concatenate([w_in[:, sl], w_in[:, 1024 + g * 128:1024 + (g + 1) * 128],
                                 w_in[:, 1536 + g * 128:1536 + (g + 1) * 128], w_in[:, 512:1024]], axis=1)
        lruvec = np.zeros((128, 16), np.float32)
        for k in range(4):
            lruvec[:, k] = conv_w[k, sl]
        lruvec[:, 4] = conv_b[sl]
        lruvec[:, 5] = b_rg[0, sl]
        lruvec[:, 6] = b_rg[1, sl]
        lruvec[:, 7] = b_ig[0, sl]
        lruvec[:, 8] = b_ig[1, sl]
        lruvec[:, 9] = lam[0, sl]
        lruvec[:, 10] = lam[1, sl]
        w_gate = np.concatenate([w_rg[0, g], w_ig[0, g], w_rg[1, g], w_ig[1, g]], axis=1)
        pr = g // 2
        slots = [g, g ^ 1, 2 * (1 - pr), 2 * (1 - pr) + 1]
        w_out_c = np.concatenate([w_out[0:512]] + [w_out[512 + 128 * s_:512 + 128 * (s_ + 1)] for s_ in slots], axis=0)
        in_maps.append({
            "xb": x[b], "ctxb": ctx[b], "cvec2": cvec2, "wada_s": wada_s, "bada_s": bada_s,
            "gaincol": col(norm_gain),
            "w_in_c": np.ascontiguousarray(w_in_c), "w_four": w_four, "w_out": np.ascontiguousarray(w_out_c), "lruvec": lruvec,
            "w_gate": np.ascontiguousarray(w_gate), "fgrow": np.ascontiguousarray(final_gain.reshape(1, D)),
            "ident": ident, "f128": f128, "f64": f64, "gtab": gtab,
        })
    return in_maps


def kernel(**inputs):
    in_maps = _prep(**inputs)
    if "nc" not in _CACHE:
        _CACHE["nc"] = build_program()
    nc = _CACHE["nc"]
    res = run_bass_kernel_spmd(nc, in_maps, core_ids=list(range(8)))
    out = np.zeros((2, SEQ, D), np.float32)
    for core in range(8):
        b, g = core // 4, core % 4
        out[b, g * OWN:(g + 1) * OWN, :] = res.results[core]["y"]
    return out
```
